# Optimizing a Trainium2 kernel written in Bass

```python
import jax, jax.numpy as jnp
from jax import lax
import numpy as np

D_MODEL = 1024
BATCH = 8
SEQ = 4096
DEPTH = 1

EPS = 1e-6
D_FF = 2816
CONV_W = 4
GDN_HEADS = 8
GDN_DK = 128
GDN_DV = 128
GDN_CHUNK = 64
MLA_HEADS = 8
Q_LORA = 384
KV_LORA = 256
QK_NOPE = 128
QK_ROPE = 64
V_HEAD = 128
ROPE_THETA = 10000.0
Q_BLOCK = 128

GDN_QK_W = GDN_HEADS * GDN_DK
GDN_V_W = GDN_HEADS * GDN_DV
MLA_V_W = MLA_HEADS * V_HEAD
IN_WIDTHS = (GDN_QK_W, GDN_QK_W, GDN_V_W, GDN_V_W, GDN_HEADS, GDN_HEADS,
             Q_LORA, KV_LORA + QK_ROPE, D_MODEL, D_MODEL)
D_IN = 2 * GDN_QK_W + 2 * GDN_V_W + 2 * GDN_HEADS + Q_LORA + KV_LORA + QK_ROPE + 2 * D_MODEL
CONV_CH = 2 * GDN_QK_W + GDN_V_W

kernel_name = "hybrid_gdn_mla_macaron"


def rms_norm(x, w):
    xf = x.astype(jnp.float32)
    xf = xf * lax.rsqrt(jnp.mean(xf * xf, axis=-1, keepdims=True) + EPS)
    return (xf * w.astype(jnp.float32)).astype(x.dtype)


def l2_norm(x):
    return x * lax.rsqrt(jnp.sum(x * x, axis=-1, keepdims=True) + EPS)


def swiglu(x, w_gate, w_up, w_down):
    return (jax.nn.silu(x @ w_gate) * (x @ w_up)) @ w_down


def causal_depthwise_conv(x, w):
    c = x.shape[-1]
    return lax.conv_general_dilated(
        x, w[:, None, :].astype(x.dtype), window_strides=(1,), padding=[(CONV_W - 1, 0)],
        dimension_numbers=('NWC', 'WIO', 'NWC'), feature_group_count=c)


def gated_delta_rule_chunked(q, k, v, g, beta):
    b, t, h, dk = q.shape
    dv = v.shape[-1]
    c = GDN_CHUNK
    n = t // c

    def chunks(a):
        return jnp.moveaxis(a.reshape((b, n, c, h) + a.shape[3:]), 3, 1)

    q, k, v, g, beta = chunks(q), chunks(k), chunks(v), chunks(g), chunks(beta)
    gc = jnp.cumsum(g, axis=-1)
    idx = jnp.arange(c)
    causal = idx[:, None] >= idx[None, :]
    strict = idx[:, None] > idx[None, :]
    decay = jnp.exp(jnp.where(causal, gc[..., :, None] - gc[..., None, :], -jnp.inf))
    kb = k * beta[..., None]
    vb = v * beta[..., None]
    lmat = jnp.where(strict, jnp.einsum('bhncd,bhnsd->bhncs', kb, k) * decay, 0.0)
    rhs = jnp.concatenate([vb, kb * jnp.exp(gc)[..., None]], axis=-1)
    sol = lax.linalg.triangular_solve(lmat, rhs, left_side=True, lower=True, unit_diagonal=True)
    u_c, w_c = sol[..., :dv], sol[..., dv:]
    attn = jnp.einsum('bhncd,bhnsd->bhncs', q, k) * decay
    q_dec = q * jnp.exp(gc)[..., None]
    k_dec = k * jnp.exp(gc[..., -1:] - gc)[..., None]
    g_tot = jnp.exp(gc[..., -1])
    xs = tuple(jnp.moveaxis(a, 2, 0) for a in (attn, u_c, w_c, q_dec, k_dec, g_tot))

    def step(state, inp):
        attn_i, u_i, w_i, qd_i, kd_i, gt_i = inp
        v_new = u_i - jnp.einsum('bhcd,bhde->bhce', w_i, state)
        o_i = jnp.einsum('bhcd,bhde->bhce', qd_i, state) + jnp.einsum('bhcs,bhse->bhce', attn_i, v_new)
        state = state * gt_i[..., None, None] + jnp.einsum('bhcd,bhce->bhde', kd_i, v_new)
        return state, o_i

    s0 = jnp.zeros((b, h, dk, dv), jnp.float32)
    _, o = lax.scan(step, s0, xs)
    o = jnp.moveaxis(o, 0, 2)
    return jnp.moveaxis(o, 1, 3).reshape(b, t, h, dv)


def gdn_branch(qa, ka, va, za, ba, aa, conv_w, a_log, dt_bias, gdn_norm, proj_a):
    b, t, _ = qa.shape
    qkv = jax.nn.silu(causal_depthwise_conv(jnp.concatenate([qa, ka, va], axis=-1), conv_w))
    q, k, v = jnp.split(qkv, [GDN_QK_W, 2 * GDN_QK_W], axis=-1)
    q = l2_norm(q.reshape(b, t, GDN_HEADS, GDN_DK).astype(jnp.float32)) * (GDN_DK ** -0.5)
    k = l2_norm(k.reshape(b, t, GDN_HEADS, GDN_DK).astype(jnp.float32))
    v = v.reshape(b, t, GDN_HEADS, GDN_DV).astype(jnp.float32)
    beta = jax.nn.sigmoid(ba.astype(jnp.float32))
    g = -jnp.exp(a_log.astype(jnp.float32)) * jax.nn.softplus(
        aa.astype(jnp.float32) + dt_bias.astype(jnp.float32))
    o = gated_delta_rule_chunked(q, k, v, g, beta)
    z = za.reshape(b, t, GDN_HEADS, GDN_DV).astype(jnp.float32)
    o = rms_norm(o, gdn_norm) * jax.nn.silu(z)
    return o.reshape(b, t, GDN_V_W).astype(qa.dtype) @ proj_a


def rope_cos_sin(positions):
    inv_freq = ROPE_THETA ** (-jnp.arange(0, QK_ROPE, 2, dtype=jnp.float32) / QK_ROPE)
    ang = positions.astype(jnp.float32)[..., None] * inv_freq
    return jnp.cos(ang), jnp.sin(ang)


def apply_rope(x, cos, sin):
    half = QK_ROPE // 2
    x1, x2 = x[..., :half], x[..., half:]
    cos = cos.astype(x.dtype)
    sin = sin.astype(x.dtype)
    return jnp.concatenate([x1 * cos - x2 * sin, x2 * cos + x1 * sin], axis=-1)


def causal_block_attention(q_nope, q_pe, k_nope, k_pe, v):
    b, t, h, _ = q_nope.shape
    nb = t // Q_BLOCK
    qn = jnp.moveaxis(q_nope.reshape(b, nb, Q_BLOCK, h, QK_NOPE), 1, 0)
    qp = jnp.moveaxis(q_pe.reshape(b, nb, Q_BLOCK, h, QK_ROPE), 1, 0)
    key_idx = jnp.arange(t)
    scale = (QK_NOPE + QK_ROPE) ** -0.5

    def block(args):
        qn_b, qp_b, blk = args
        s = (jnp.einsum('bqhd,bkhd->bhqk', qn_b, k_nope)
             + jnp.einsum('bqhr,bkr->bhqk', qp_b, k_pe)).astype(jnp.float32) * scale
        q_idx = blk * Q_BLOCK + jnp.arange(Q_BLOCK)
        s = jnp.where(key_idx[None, :] <= q_idx[:, None], s, -jnp.inf)
        p = jax.nn.softmax(s, axis=-1).astype(v.dtype)
        return jnp.einsum('bhqk,bkhd->bqhd', p, v)

    o = lax.map(block, (qn, qp, jnp.arange(nb)))
    return jnp.moveaxis(o, 0, 1).reshape(b, t, h, V_HEAD)


def mla_branch(qd, kvd, positions, q_a_norm, w_q_up, kv_a_norm, w_kv_up, proj_b):
    b, t, _ = qd.shape
    q = (rms_norm(qd, q_a_norm) @ w_q_up).reshape(b, t, MLA_HEADS, QK_NOPE + QK_ROPE)
    q_nope, q_pe = q[..., :QK_NOPE], q[..., QK_NOPE:]
    c_kv, k_pe = kvd[..., :KV_LORA], kvd[..., KV_LORA:]
    kv = (rms_norm(c_kv, kv_a_norm) @ w_kv_up).reshape(b, t, MLA_HEADS, QK_NOPE + V_HEAD)
    k_nope, v = kv[..., :QK_NOPE], kv[..., QK_NOPE:]
    cos, sin = rope_cos_sin(positions)
    q_pe = apply_rope(q_pe, cos[:, :, None, :], sin[:, :, None, :])
    k_pe = apply_rope(k_pe, cos, sin)
    o = causal_block_attention(q_nope, q_pe, k_nope, k_pe, v)
    return o.reshape(b, t, MLA_V_W) @ proj_b


def hybrid_mixer(u, positions, w_in, conv_w, a_log, dt_bias, gdn_norm, proj_a,
                 q_a_norm, w_q_up, kv_a_norm, w_kv_up, proj_b, w_o):
    proj = u @ w_in
    offsets = [int(o) for o in np.cumsum(IN_WIDTHS)[:-1]]
    qa, ka, va, za, ba, aa, qd, kvd, gate_a, gate_b = jnp.split(proj, offsets, axis=-1)
    y_a = gdn_branch(qa, ka, va, za, ba, aa, conv_w, a_log, dt_bias, gdn_norm, proj_a)
    y_b = mla_branch(qd, kvd, positions, q_a_norm, w_q_up, kv_a_norm, w_kv_up, proj_b)
    merged = jax.nn.sigmoid(gate_a) * y_a + jax.nn.sigmoid(gate_b) * y_b
    return merged @ w_o


def setup_inputs(seed: int = 0) -> dict:
    key = jax.random.key(seed)
    ks = jax.random.split(key, 26)

    def dense(k, fan_in, fan_out):
        return jax.random.normal(k, (DEPTH, fan_in, fan_out), jnp.float32) * fan_in ** -0.5

    def gain(k, n):
        return 1.0 + 0.02 * jax.random.normal(k, (DEPTH, n), jnp.float32)

    x = jax.random.normal(ks[0], (BATCH, SEQ, D_MODEL), jnp.float32)
    positions = (jax.random.randint(ks[1], (BATCH, 1), 0, 4096, dtype=jnp.int32)
                 + jnp.arange(SEQ, dtype=jnp.int32)[None, :])
    a_log = jnp.log(jax.random.uniform(ks[9], (DEPTH, GDN_HEADS), jnp.float32, 1.0, 16.0))
    dt = jnp.exp(jax.random.uniform(ks[10], (DEPTH, GDN_HEADS), jnp.float32,
                                    np.log(0.001).astype(np.float32), np.log(0.1).astype(np.float32)))
    dt_bias = dt + jnp.log(-jnp.expm1(-dt))
    return {
        "x": x,
        "positions": positions,
        "ffn1_norm": gain(ks[2], D_MODEL),
        "ffn1_w_gate": dense(ks[3], D_MODEL, D_FF),
        "ffn1_w_up": dense(ks[4], D_MODEL, D_FF),
        "ffn1_w_down": dense(ks[5], D_FF, D_MODEL),
        "mix_norm": gain(ks[6], D_MODEL),
        "w_in": dense(ks[7], D_MODEL, D_IN),
        "conv_w": jax.random.normal(ks[8], (DEPTH, CONV_W, CONV_CH), jnp.float32) * CONV_W ** -0.5,
        "a_log": a_log,
        "dt_bias": dt_bias,
        "gdn_norm": gain(ks[11], GDN_DV),
        "proj_a": dense(ks[12], GDN_V_W, D_MODEL),
        "q_a_norm": gain(ks[13], Q_LORA),
        "w_q_up": dense(ks[14], Q_LORA, MLA_HEADS * (QK_NOPE + QK_ROPE)),
        "kv_a_norm": gain(ks[15], KV_LORA),
        "w_kv_up": dense(ks[16], KV_LORA, MLA_HEADS * (QK_NOPE + V_HEAD)),
        "proj_b": dense(ks[17], MLA_V_W, D_MODEL),
        "w_o": dense(ks[18], D_MODEL, D_MODEL),
        "ffn2_norm": gain(ks[19], D_MODEL),
        "ffn2_w_gate": dense(ks[20], D_MODEL, D_FF),
        "ffn2_w_up": dense(ks[21], D_MODEL, D_FF),
        "ffn2_w_down": dense(ks[22], D_FF, D_MODEL),
        "final_norm": 1.0 + 0.02 * jax.random.normal(ks[23], (D_MODEL,), jnp.float32),
    }


def reference(x, positions, ffn1_norm, ffn1_w_gate, ffn1_w_up, ffn1_w_down, mix_norm, w_in,
              conv_w, a_log, dt_bias, gdn_norm, proj_a, q_a_norm, w_q_up, kv_a_norm, w_kv_up,
              proj_b, w_o, ffn2_norm, ffn2_w_gate, ffn2_w_up, ffn2_w_down, final_norm):
    h = x
    for l in range(DEPTH):
        h = h + 0.5 * swiglu(rms_norm(h, ffn1_norm[l]), ffn1_w_gate[l], ffn1_w_up[l], ffn1_w_down[l])
        u = rms_norm(h, mix_norm[l])
        h = h + hybrid_mixer(u, positions, w_in[l], conv_w[l], a_log[l], dt_bias[l], gdn_norm[l],
                             proj_a[l], q_a_norm[l], w_q_up[l], kv_a_norm[l], w_kv_up[l],
                             proj_b[l], w_o[l])
        h = h + 0.5 * swiglu(rms_norm(h, ffn2_norm[l]), ffn2_w_gate[l], ffn2_w_up[l], ffn2_w_down[l])
    return rms_norm(h, final_norm)
```

```python
import numpy as np
from contextlib import ExitStack
import concourse.bass as bass
import concourse.mybir as mybir
from concourse.bass_utils import run_bass_kernel_spmd

F32 = mybir.dt.float32
BF16 = mybir.dt.bfloat16
I32 = mybir.dt.int32
AF = mybir.ActivationFunctionType
ALU = mybir.AluOpType

D = 1024
T = 4096
DFF = 2816
NF = DFF // 128
TG = 512
NG = T // TG
EPS = 1e-6
NH = 8
QL = 384
KVL = 256
SCALE = float((128 + 64) ** -0.5)
PI = float(np.pi)
OWN_SKIP = ("pe",)
GDN_BAL = 1
GDN_STAGGER = (7, 6, 5, 4, 3, 2, 1, 0)
ATT_P1 = 4
ATT_RR = 1
MIXER_ENABLED = True


def _unit_cols(w, cols):
    k = w.shape[0]
    sub = w[:, cols]
    return sub.reshape(k // 128, 128, len(cols)).transpose(1, 0, 2).reshape(128, -1)


class Layout:
    def __init__(self):
        self.off = {}
        self.parts = []
        self.n = 0

    def add(self, name, arr):
        arr = np.ascontiguousarray(arr, dtype=np.float32)
        assert arr.shape[0] == 128
        self.off[name] = (self.n, arr.shape[1])
        self.parts.append(arr)
        self.n += arr.shape[1]


def build_layout(inp, shapes_only=False):
    L = Layout()

    def W(name):
        a = inp[name]
        return a[0]

    def add_ffn(pre):
        wg, wu, wd = W(pre + "_w_gate"), W(pre + "_w_up"), W(pre + "_w_down")
        for f in range(NF):
            cols = np.arange(f * 128, (f + 1) * 128)
            g = _unit_cols(wg, cols).reshape(128, 8, 1, 128)
            u = _unit_cols(wu, cols).reshape(128, 8, 1, 128)
            L.add(f"{pre}_gu{f}", np.concatenate([g, u], axis=2).reshape(128, -1))
        for c in range(8):
            cols = np.arange(c * 128, (c + 1) * 128)
            L.add(f"{pre}_d{c}", _unit_cols(wd, cols))

    add_ffn("ffn1")
    win = W("w_in")
    names = ["q", "k", "v", "z"]
    for s in range(4):
        for h in range(8):
            c0 = s * 1024 + h * 128
            L.add(f"in_{names[s]}{h}", _unit_cols(win, np.arange(c0, c0 + 128)))
    L.add("in_bg", _unit_cols(win, np.arange(4096, 4112)))
    for i in range(3):
        L.add(f"in_qd{i}", _unit_cols(win, np.arange(4112 + i * 128, 4112 + (i + 1) * 128)))
    for i in range(2):
        L.add(f"in_ckv{i}", _unit_cols(win, np.arange(4496 + i * 128, 4496 + (i + 1) * 128)))
    kp = np.arange(4752, 4816)
    kps = np.concatenate([kp[32:], kp[:32]])
    L.add("in_kpe", np.concatenate([_unit_cols(win, kp).reshape(128, 8, 64),
                                    _unit_cols(win, kps).reshape(128, 8, 64)], axis=2).reshape(128, -1))
    for i in range(8):
        L.add(f"in_ga{i}", _unit_cols(win, np.arange(4816 + i * 128, 4816 + (i + 1) * 128)))
    for i in range(8):
        L.add(f"in_gb{i}", _unit_cols(win, np.arange(5840 + i * 128, 5840 + (i + 1) * 128)))
    wq = W("w_q_up")
    for h in range(8):
        b = h * 192
        nope = np.arange(b, b + 128)
        r = np.arange(b + 128, b + 192)
        rs = np.concatenate([r[32:], r[:32]])
        L.add(f"q_up{h}", np.concatenate([_unit_cols(wq, nope).reshape(128, 3, 128),
                                          _unit_cols(wq, r).reshape(128, 3, 64),
                                          _unit_cols(wq, rs).reshape(128, 3, 64)], axis=2).reshape(128, -1))
    wkv = W("w_kv_up")
    kcols = np.concatenate([np.arange(h * 256, h * 256 + 128) for h in range(8)])
    vcols = np.concatenate([np.arange(h * 256 + 128, h * 256 + 256) for h in range(8)])
    L.add("kv_k", _unit_cols(wkv, kcols))
    L.add("kv_v", _unit_cols(wkv, vcols))
    for nm in ("proj_a", "proj_b", "w_o"):
        w = W(nm)
        for c in range(8):
            L.add(f"{nm}{c}", _unit_cols(w, np.arange(c * 128, (c + 1) * 128)))
    L.split = L.n
    add_ffn("ffn2")
    return L


VEC = {}


def build_vecs(inp):
    cols = []

    def add(name, a):
        a = np.asarray(a, np.float32)
        if a.ndim == 1:
            a = a[:, None]
        VEC[name] = (sum(c.shape[1] for c in cols), a.shape[1])
        cols.append(a)

    def chunks(v):
        return v.reshape(-1, 128).T

    add("ffn1_norm", chunks(inp["ffn1_norm"][0]))
    add("mix_norm", chunks(inp["mix_norm"][0]))
    add("ffn2_norm", chunks(inp["ffn2_norm"][0]))
    add("final_norm", chunks(inp["final_norm"]))
    cw = inp["conv_w"][0]
    add("conv", np.concatenate([chunks(cw[j]) for j in range(4)], axis=1))
    add("gdn_norm", inp["gdn_norm"][0])
    add("q_a_norm", chunks(inp["q_a_norm"][0]))
    add("kv_a_norm", chunks(inp["kv_a_norm"][0]))
    add("a_log", np.broadcast_to(inp["a_log"][0][None, :], (128, 8)))
    add("dt_bias", np.broadcast_to(inp["dt_bias"][0][None, :], (128, 8)))
    inv = (10000.0 ** (-np.arange(0, 64, 2, dtype=np.float32) / 64)).astype(np.float32)
    f2 = np.zeros(128, np.float32)
    f2[:64] = np.concatenate([inv, inv])
    add("invfreq", f2)
    sg = np.zeros(128, np.float32)
    sg[:32] = -1.0
    sg[32:64] = 1.0
    add("sgn", sg)
    return np.ascontiguousarray(np.concatenate(cols, axis=1))


def build_consts():
    i = np.arange(128)
    ident = np.eye(128, dtype=np.float32)
    ones = np.ones((128, 128), np.float32)
    tri = (i[:, None] <= i[None, :]).astype(np.float32)
    su = (i[:, None] > i[None, :]).astype(np.float32)
    mincl = (i[None, :] >= i[:, None]).astype(np.float32)
    mstr = (i[None, :] > i[:, None]).astype(np.float32)
    negm = np.where(i[:, None] > i[None, :], -30000.0, 0.0).astype(np.float32)
    return np.ascontiguousarray(np.concatenate([ident, ones, tri, su, mincl, mstr, negm], axis=1))


class Mem:
    def __init__(self, size):
        self.recs = [[0, size, None, {}]]

    def deps(self, lo, hi, write, out):
        for r in self.recs:
            if r[0] < hi and lo < r[1]:
                if r[2] is not None:
                    s, v = r[2]
                    if out.get(s, 0) < v:
                        out[s] = v
                if write:
                    for s, v in r[3].items():
                        if out.get(s, 0) < v:
                            out[s] = v

    def commit(self, lo, hi, write, dep):
        new = []
        for r in self.recs:
            if r[0] < hi and lo < r[1]:
                if r[0] < lo:
                    new.append([r[0], lo, r[2], dict(r[3])])
                if r[1] > hi:
                    new.append([hi, r[1], r[2], dict(r[3])])
                if not write:
                    rd = dict(r[3])
                    if rd.get(dep[0], 0) < dep[1]:
                        rd[dep[0]] = dep[1]
                    new.append([max(r[0], lo), min(r[1], hi), r[2], rd])
            else:
                new.append(r)
        if write:
            new.append([lo, hi, dep, {}])
        self.recs = new


ES = {F32: 4, BF16: 2, I32: 4}


class V:
    def __init__(self, ap, mem, lo, hi, dt, whole=False):
        self.ap, self.mem, self.lo, self.hi, self.dt, self.whole = ap, mem, lo, hi, dt, whole

    def c(self, a, b):
        e = ES[self.dt]
        if self.whole:
            return V(self.ap[:, a:b], self.mem, self.lo, self.hi, self.dt, True)
        return V(self.ap[:, a:b], self.mem, self.lo + a * e, self.lo + b * e, self.dt)

    def p(self, a, b):
        return V(self.ap[a:b], self.mem, self.lo, self.hi, self.dt, self.whole)

    def pc(self, p0, p1, a, b):
        e = ES[self.dt]
        if self.whole:
            return V(self.ap[p0:p1, a:b], self.mem, self.lo, self.hi, self.dt, True)
        return V(self.ap[p0:p1, a:b], self.mem, self.lo + a * e, self.lo + b * e, self.dt)


class Eng:
    def __init__(self, h, sem, sid):
        self.h, self.sem, self.sid = h, sem, sid
        self.count = 0
        self.known = {}
        self.dsems = []
        self.dma_i = 0


class KB:
    def __init__(self, nc, es):
        self.nc = nc
        self.es = es
        self.sems = {}
        self.eng = {}
        for name, h in (("pe", nc.tensor), ("act", nc.scalar), ("dve", nc.vector),
                        ("pool", nc.gpsimd), ("sp", nc.sync)):
            sem = es.enter_context(nc.semaphore("s_" + name))
            self.sems[name] = sem
            self.eng[name] = Eng(h, sem, name)
        for q in ("sp", "pool", "act"):
            for i in range(8):
                sid = f"d_{q}{i}"
                sem = es.enter_context(nc.semaphore(sid))
                self.sems[sid] = sem
                self.eng[q].dsems.append(sid)
        self.nins = 0

    def sb(self, name, cols, dt, parts=128):
        t = self.es.enter_context(self.nc.sbuf_tensor(name, [parts, cols], dt))
        return V(t[:], Mem(cols * ES[dt]), 0, cols * ES[dt], dt)

    def _waits(self, E, reads, writes, extra=None):
        deps = {}
        for v in reads:
            v.mem.deps(v.lo, v.hi, v.whole, deps)
        for v in writes:
            v.mem.deps(v.lo, v.hi, True, deps)
        if extra:
            for s, val in extra.items():
                if deps.get(s, 0) < val:
                    deps[s] = val
        for s, val in deps.items():
            if s == E.sid and E.sid in OWN_SKIP:
                continue
            if E.known.get(s, 0) < val:
                E.h.wait_ge(self.sems[s], val)
                E.known[s] = val
                self.nins += 1

    def op(self, eng, fn, reads, writes):
        E = self.eng[eng]
        self._waits(E, reads, writes)
        ins = fn(E.h)
        E.count += 1
        ins.then_inc(E.sem, 1)
        dep = (E.sid, E.count)
        for v in reads:
            v.mem.commit(v.lo, v.hi, v.whole, dep)
        for v in writes:
            v.mem.commit(v.lo, v.hi, True, dep)
        self.nins += 1

    def dma(self, q, out_v, in_v, out_ap=None, in_ap=None, **kw):
        E = self.eng[q]
        R = len(E.dsems)
        sid = E.dsems[E.dma_i % R]
        uses = E.dma_i // R
        E.dma_i += 1
        self._waits(E, [in_v], [out_v], extra={sid: 16 * uses} if uses else None)
        ins = E.h.dma_start(out=out_ap if out_ap is not None else out_v.ap,
                            in_=in_ap if in_ap is not None else in_v.ap, **kw)
        ins.then_inc(self.sems[sid], 16)
        dep = (sid, 16 * (uses + 1))
        in_v.mem.commit(in_v.lo, in_v.hi, False, dep)
        out_v.mem.commit(out_v.lo, out_v.hi, True, dep)
        self.nins += 1
        return dep

    def wait_all(self, q, deps):
        E = self.eng[q]
        for s, val in deps:
            if E.known.get(s, 0) < val:
                E.h.wait_ge(self.sems[s], val)
                E.known[s] = val


class Arena:
    def __init__(self, kb, name, nbytes):
        self.t = kb.es.enter_context(kb.nc.sbuf_tensor(name, [128, nbytes // 2], BF16))
        self.mem = Mem(nbytes)
        self.top = 0
        self.nbytes = nbytes

    def alloc(self, cols, dt):
        nb = cols * ES[dt]
        nb_al = (nb + 63) // 64 * 64
        lo = self.top
        assert lo + nb_al <= self.nbytes, ("arena overflow", lo, nb_al, self.nbytes)
        self.top += nb_al
        ap = self.t[:, lo // 2:(lo + nb) // 2]
        if dt != BF16:
            ap = ap.bitcast(dt)
        return V(ap, self.mem, lo, lo + nb, dt)


def build_program(lay_off, ltot, nvec, dbg=None, ngroups=NG, stop=99, lsplit=None):
    lsplit = ltot if lsplit is None else lsplit
    nc = bass.Bass("TRN2", target_bir_lowering=False)
    x_d = nc.dram_tensor("x", [T, D], F32, kind="ExternalInput").ap()
    pos_d = nc.dram_tensor("pos", [64, T], I32, kind="ExternalInput").ap()
    w32_d = nc.dram_tensor("w32", [128, ltot], F32, kind="ExternalInput").ap()
    vec_d = nc.dram_tensor("vecs", [128, nvec], F32, kind="ExternalInput").ap()
    cst_d = nc.dram_tensor("consts", [128, 7 * 128], F32, kind="ExternalInput").ap()
    out_d = nc.dram_tensor("out", [T, D], F32, kind="ExternalOutput").ap()
    wbf_d = nc.dram_tensor("wbf", [128, ltot], BF16).ap()
    kc_d = nc.dram_tensor("kcache", [NH, 128, T], BF16).ap()
    vc_d = nc.dram_tensor("vcache", [NH, 128, T // 128, 128], BF16).ap()
    dbg_d = {}
    if dbg:
        for nm, shp in dbg.items():
            dbg_d[nm] = nc.dram_tensor("dbg_" + nm, list(shp), F32, kind="ExternalOutput").ap()

    with ExitStack() as es:
        kb = KB(nc, es)
        op, dma = kb.op, kb.dma
        w32_v = V(w32_d, Mem(ltot), 0, ltot, F32)
        wbf_m = Mem(ltot)
        x_m = Mem(T)
        out_m = Mem(T)
        kc_m = [Mem(T) for _ in range(NH)]
        vc_m = [Mem(T // 128) for _ in range(NH)]
        misc_m = Mem(16)

        def dv(ap, mem, lo, hi, dt=F32):
            return V(ap, mem, lo, hi, dt)

        cst = kb.sb("cst", 7 * 128, F32)
        cstb = kb.sb("cstb", 3 * 128, BF16)
        vec = kb.sb("vec", nvec, F32)
        hT = kb.sb("hT", 8 * TG, F32)
        uT = kb.sb("uT", 8 * TG, BF16)
        RS = 2816
        NSLOT = 5
        ring = [kb.sb(f"ring{i}", RS, BF16) for i in range(NSLOT)]
        halo = kb.sb("halo", 24 * 3, F32)
        Sst = kb.sb("Sst", NH * 128, F32)
        Sbf = kb.sb("Sbf", NH * 128, BF16)
        kpe = kb.sb("kpe", T, BF16, parts=64)
        c2 = kb.sb("c2", TG, F32, parts=64)
        s2 = kb.sb("s2", TG, F32, parts=64)
        nega = kb.sb("nega", 8, F32)
        neghalf = kb.sb("neghalf", 1, F32)
        AR = Arena(kb, "arena", 128 * 1024)
        nsq = [kb.sb("nsq0", TG, BF16), kb.sb("nsq1", TG, BF16)]
        ps = []
        for i in range(8):
            t = es.enter_context(nc.psum_tensor(f"ps{i}", [128, 512], F32))
            ps.append(V(t[:], Mem(2048), 0, 2048, F32, True))

        def psb(i):
            v = ps[i]
            return V(v.ap.bitcast(BF16), v.mem, 0, 2048, BF16, True)

        ident_f = cst.c(0, 128)
        ones_f = cst.c(128, 256)
        tri_f = cst.c(256, 384)
        su_f = cst.c(384, 512)
        mincl_f = cst.c(512, 640)
        mstr_f = cst.c(640, 768)
        ident_b = cstb.c(0, 128)
        ones_b = cstb.c(128, 256)
        negm_b = cstb.c(256, 384)

        def vcol(name, i=0, n=1):
            o, _ = VEC[name]
            return vec.c(o + i, o + i + n)

        dma("sp", cst, dv(cst_d, misc_m, 0, 1))
        dma("sp", vec, dv(vec_d, misc_m, 1, 2))
        def convert(a, end):
            while a < end:
                b = min(end, a + 2048)
                dma("pool", V(wbf_d[:, a:b], wbf_m, a, b, BF16), V(w32_d[:, a:b], w32_v.mem, a, b, F32))
                a = b

        convert(0, lsplit)
        op("dve", lambda e: e.tensor_copy(ident_b.ap, ident_f.ap), [ident_f], [ident_b])
        op("dve", lambda e: e.tensor_copy(ones_b.ap, ones_f.ap), [ones_f], [ones_b])
        op("dve", lambda e: e.tensor_copy(negm_b.ap, cst.c(768, 896).ap), [cst], [negm_b])
        op("dve", lambda e: e.memset(halo.ap, 0.0), [], [halo])
        op("dve", lambda e: e.memset(neghalf.ap, -0.5), [], [neghalf])
        op("dve", lambda e: e.memset(Sst.ap, 0.0), [], [Sst])
        op("dve", lambda e: e.memset(Sbf.ap, 0.0), [], [Sbf])
        op("act", lambda e: e.activation(out=nega.ap, in_=vcol("a_log", 0, 8).ap, func=AF.Exp), [vec], [nega])
        op("dve", lambda e: e.tensor_scalar(nega.ap, nega.ap, -1.0, None, ALU.mult), [nega], [nega])

        ring_i = [0]

        def load_unit(name):
            off, L = lay_off[name]
            slot = ring[ring_i[0] % NSLOT]
            ring_i[0] += 1
            dst = slot.c(0, L)
            dma("sp", dst, V(wbf_d[:, off:off + L], wbf_m, off, off + L, BF16))
            return dst

        rr = {"a": 0}

        def dbg_store(name, src, rows=None, cols=None):
            if name not in dbg_d:
                return
            d = dbg_d[name]
            r0, r1 = rows if rows else (0, d.shape[0])
            c0, c1 = cols if cols else (0, d.shape[1])
            dma("pool", V(d[r0:r1, c0:c1], misc_m, 2, 3, F32), src)

        def rmsnorm_T(src_chunks, nfeat, gain_name, dst_chunks, base, pre=False):
            nchunk = len(src_chunks)
            sr = AR.alloc(TG, F32)
            rstd = AR.alloc(TG, F32)
            acc = ps[5]
            if not pre:
                sq = [AR.alloc(TG, BF16) for _ in range(2)]
                for c in range(nchunk):
                    s = sq[c % 2]
                    op("act", lambda e, s=s, c=c: e.activation(out=s.ap, in_=src_chunks[c].ap, func=AF.Square),
                       [src_chunks[c]], [s])
                    op("pe", lambda e, s=s, c=c: e.matmul(acc.ap, ones_b.ap, s.ap, start=(c == 0), stop=(c == nchunk - 1)),
                       [ones_b, s], [acc])
            op("act", lambda e: e.activation(out=sr.ap, in_=acc.ap, func=AF.Ln, bias=EPS, scale=1.0 / nfeat),
               [acc], [sr])
            op("act", lambda e: e.activation(out=rstd.ap, in_=sr.ap, func=AF.Exp, scale=-0.5), [sr], [rstd])
            for c in range(nchunk):
                g = vcol(gain_name, c)
                op("dve", lambda e, c=c, g=g: e.scalar_tensor_tensor(dst_chunks[c].ap, src_chunks[c].ap, g.ap, rstd.ap,
                                                                    ALU.mult, ALU.mult),
                   [src_chunks[c], vec, rstd], [dst_chunks[c]])
            return rstd

        hTc = [hT.c(c * TG, (c + 1) * TG) for c in range(8)]
        uTc = [uT.c(c * TG, (c + 1) * TG) for c in range(8)]

        def norm_acc_sq(c):
            s_ = nsq[c % 2]
            op("act", lambda e: e.activation(out=s_.ap, in_=hTc[c].ap, func=AF.Square), [hTc[c]], [s_])

        def norm_acc_pe(c):
            s_ = nsq[c % 2]
            op("pe", lambda e: e.matmul(ps[5].ap, ones_b.ap, s_.ap, start=(c == 0), stop=(c == 7)), [ones_b, s_], [ps[5]])

        def ffn(pre, norm_name, pre_acc=False):
            top = AR.top
            rmsnorm_T(hTc, D, norm_name, uTc, 0, pre=pre_acc)
            aT = [AR.alloc(TG, BF16) for _ in range(NF)]
            sg = [AR.alloc(TG, F32) for _ in range(2)]
            for f in range(NF):
                u = load_unit(f"{pre}_gu{f}")
                pg, pu = ps[(f % 2) * 2], ps[(f % 2) * 2 + 1]
                for kc in range(8):
                    wg = u.c(kc * 256, kc * 256 + 128)
                    op("pe", lambda e, wg=wg, kc=kc, pg=pg: e.matmul(pg.ap, wg.ap, uTc[kc].ap, start=(kc == 0), stop=(kc == 7)),
                       [wg, uTc[kc]], [pg])
                for kc in range(8):
                    wu = u.c(kc * 256 + 128, kc * 256 + 256)
                    op("pe", lambda e, wu=wu, kc=kc, pu=pu: e.matmul(pu.ap, wu.ap, uTc[kc].ap, start=(kc == 0), stop=(kc == 7)),
                       [wu, uTc[kc]], [pu])
                s = sg[f % 2]
                op("act", lambda e, s=s, pg=pg: e.activation(out=s.ap, in_=pg.ap, func=AF.Silu), [pg], [s])
                op("dve", lambda e, s=s, pu=pu, f=f: e.tensor_tensor(aT[f].ap, pu.ap, s.ap, ALU.mult), [pu, s], [aT[f]])
            for c in range(8):
                u = load_unit(f"{pre}_d{c}")
                py = ps[4] if c % 2 == 0 else ps[6]
                for f in range(NF):
                    wd = u.c(f * 128, (f + 1) * 128)
                    op("pe", lambda e, wd=wd, f=f, py=py: e.matmul(py.ap, wd.ap, aT[f].ap, start=(f == 0), stop=(f == NF - 1)),
                       [wd, aT[f]], [py])
                op("dve", lambda e, c=c, py=py: e.scalar_tensor_tensor(hTc[c].ap, py.ap, 0.5, hTc[c].ap, ALU.mult, ALU.add),
                   [py, hTc[c]], [hTc[c]])
                if c >= 1:
                    norm_acc_pe(c - 1)
                norm_acc_sq(c)
            norm_acc_pe(7)
            AR.top = top

        def proj_chunk(unit, nk, rhs_chunks, pout, mcols=None, ucol0=0, ustride=None, n=TG, r0=0):
            m = mcols if mcols else 128
            st = ustride if ustride else m
            for kc in range(nk):
                w = unit.c(kc * st + ucol0, kc * st + ucol0 + m)
                rhs = rhs_chunks[kc].c(r0, r0 + n)
                po = pout.pc(0, m, 0, n)
                op("pe", lambda e, w=w, rhs=rhs, kc=kc, po=po: e.matmul(po.ap, w.ap, rhs.ap, start=(kc == 0), stop=(kc == nk - 1)),
                   [w, rhs], [pout])

        def cp(eng, dst, src, rd=None, wr=None):
            rd = rd if rd is not None else [src]
            wr = wr if wr is not None else [dst]
            if eng == "act":
                op("act", lambda e: e.copy(dst.ap, src.ap), rd, wr)
            elif eng == "dve":
                op("dve", lambda e: e.tensor_copy(dst.ap, src.ap), rd, wr)
            else:
                op("pool", lambda e: e.tensor_copy(dst.ap, src.ap), rd, wr)

        def mm(out, lhsT, rhs, start=True, stop=True):
            op("pe", lambda e: e.matmul(out.ap, lhsT.ap, rhs.ap, start=start, stop=stop), [lhsT, rhs], [out])

        def tt(eng, out, a, b, alu):
            op(eng, lambda e: e.tensor_tensor(out.ap, a.ap, b.ap, alu), [a, b], [out])

        def ts(eng, out, a, s1, s2, o0, o1=None):
            rd = [a] + [s for s in (s1, s2) if isinstance(s, V)]
            a1 = s1.ap if isinstance(s1, V) else s1
            a2 = s2.ap if isinstance(s2, V) else s2
            if o1 is None:
                op(eng, lambda e: e.tensor_scalar(out.ap, a.ap, a1, a2, o0), rd, [out])
            else:
                op(eng, lambda e: e.tensor_scalar(out.ap, a.ap, a1, a2, o0, o1), rd, [out])

        def stt(out, a, s, b, o0, o1):
            rd = [a, b] + ([s] if isinstance(s, V) else [])
            sa = s.ap if isinstance(s, V) else s
            op("dve", lambda e: e.scalar_tensor_tensor(out.ap, a.ap, sa, b.ap, o0, o1), rd, [out])

        def act(out, a, func, bias=None, scale=None, accum=None):
            rd = [a] + [s for s in (bias, scale) if isinstance(s, V)]
            wr = [out] + ([accum] if accum is not None else [])
            kw = {}
            if bias is not None:
                kw["bias"] = bias.ap if isinstance(bias, V) else bias
            if scale is not None:
                kw["scale"] = scale.ap if isinstance(scale, V) else scale
            if accum is not None:
                kw["accum_out"] = accum.ap
            op("act", lambda e: e.activation(out=out.ap, in_=a.ap, func=func, **kw), rd, wr)

        def mixer(g, t0):
            top0 = AR.top
            rmsnorm_T(hTc, D, "mix_norm", uTc, 0, pre=True)
            AR.top = top0
            oTn = [AR.alloc(TG, BF16) for _ in range(NH)]
            top1 = AR.top
            P64 = lambda v: v.p(0, 64)
            beta = AR.alloc(32, F32)
            nbeta = AR.alloc(32, F32)
            xa = AR.alloc(32, F32)
            ee = AR.alloc(32, F32)
            spv = AR.alloc(32, F32)
            gv = AR.alloc(32, F32)
            gc = AR.alloc(32, F32)
            gt = AR.alloc(32, F32)
            eg = AR.alloc(32, F32)
            egl = AR.alloc(32, F32)
            egt = AR.alloc(32, F32)
            dl = AR.alloc(32, F32)
            ubg = load_unit("in_bg")
            dtb = vcol("dt_bias", 0, 8)
            for t in range(4):
                pb = ps[0].c(0, 16)
                for kc in range(8):
                    mm(pb, uTc[kc].c(t * 128, (t + 1) * 128), ubg.c(kc * 16, kc * 16 + 16), kc == 0, kc == 7)
                act(beta.c(t * 8, t * 8 + 8), pb.c(0, 8), AF.Sigmoid)
                tt("dve", xa.c(t * 8, t * 8 + 8), pb.c(8, 16), dtb, ALU.add)
            act(ee, xa, AF.Exp)
            act(spv, ee, AF.Ln, bias=1.0, scale=1.0)
            for t in range(4):
                tt("dve", gv.c(t * 8, t * 8 + 8), spv.c(t * 8, t * 8 + 8), nega, ALU.mult)
            ts("dve", nbeta, beta, -1.0, None, ALU.mult)
            for t in range(4):
                pg = ps[1].c(32, 40)
                mm(pg, tri_f, gv.c(t * 8, t * 8 + 8))
                cp("dve", gc.c(t * 8, t * 8 + 8), pg)
                pt = ps[2].c(64, 72)
                mm(pt, ones_f, gv.c(t * 8, t * 8 + 8))
                cp("dve", gt.c(t * 8, t * 8 + 8), pt)
            act(eg, gc, AF.Exp)
            tt("dve", dl, gt, gc, ALU.subtract)
            act(egl, dl, AF.Exp)
            act(egt, gt, AF.Exp)

            NI = 8
            NSET = 6
            sets = []
            for _i in range(NSET):
                sets.append({"cacc": AR.alloc(TG, F32), "hb": AR.alloc(6, F32)})
            nsets = [{"sqb": AR.alloc(TG, BF16), "rs": AR.alloc(TG, F32), "srr": AR.alloc(TG, F32)} for _ in range(2)]
            p1 = {"bank": 0, "set": 0, "nset": 0}

            def p1_bank():
                p1["bank"] += 1
                return ps[p1["bank"] % 8]

            def p1_set():
                p1["set"] += 1
                return sets[p1["set"] % NSET]

            def p1_nset():
                p1["nset"] += 1
                return nsets[p1["nset"] % 2]
            slots = []
            for s in range(NI):
                B = {}
                for nm in ("QnT", "KnT", "VT", "zs"):
                    B[nm] = AR.alloc(TG, BF16)
                for nm in ("Kg", "Kdec", "Vt", "attnT", "Ybf", "w0T", "vnew", "on"):
                    B[nm] = AR.alloc(128, BF16)
                for nm in ("dinc", "dstr", "u0b", "o1", "oo"):
                    B[nm] = AR.alloc(128, F32)
                B["Gm"] = B["dstr"]
                B["decT"] = B["dinc"]
                B["junk"] = B["o1"]
                for nm in ("W", "WT", "Y"):
                    B[nm] = [AR.alloc(128, F32) for _ in range(2)]
                for nm in ("ms", "msr", "rst"):
                    B[nm] = AR.alloc(1, F32)
                B["X"] = ps[s]
                slots.append(B)

            def gdn_phase1(B, h):
                o_c, _ = VEC["conv"]
                S3 = [p1_set(), p1_set(), p1_set()]
                X3 = [p1_bank(), p1_bank(), p1_bank()]
                outs = [B["QnT"], B["KnT"], B["VT"]]
                w3 = []
                for s, nm in enumerate("qkv"):
                    u = load_unit(f"in_{nm}{h}")
                    proj_chunk(u, 8, uTc, X3[s])
                    ch = s * 8 + h
                    w3.append([vec.c(o_c + j * 24 + ch, o_c + j * 24 + ch + 1) for j in range(4)])
                for s in range(3):
                    ch = s * 8 + h
                    hl = halo.c(ch * 3, ch * 3 + 3)
                    hb = S3[s]["hb"]
                    cp("act", hb.c(0, 3), hl)
                    cp("act", hb.c(3, 6), X3[s].c(0, 3))
                    cp("act", hl, X3[s].c(TG - 3, TG))
                for j in range(4):
                    for s in range(3):
                        cacc_ = S3[s]["cacc"]
                        if j == 0:
                            ts("dve", cacc_.c(3, TG), X3[s].c(0, TG - 3), w3[s][0], None, ALU.mult)
                        else:
                            stt(cacc_.c(3, TG), X3[s].c(j, j + TG - 3), w3[s][j], cacc_.c(3, TG), ALU.mult, ALU.add)
                for j in range(4):
                    for s in range(3):
                        cacc_, hb = S3[s]["cacc"], S3[s]["hb"]
                        if j == 0:
                            ts("dve", cacc_.c(0, 3), hb.c(0, 3), w3[s][0], None, ALU.mult)
                        else:
                            stt(cacc_.c(0, 3), hb.c(j, j + 3), w3[s][j], cacc_.c(0, 3), ALU.mult, ALU.add)
                B["p1"] = (S3, outs)

            def gdn_phase1b(B, h):
                S3, outs = B["p1"]
                for s in range(3):
                    act(outs[s], S3[s]["cacc"], AF.Silu)
                Xz = p1_bank()
                u = load_unit(f"in_z{h}")
                proj_chunk(u, 8, uTc, Xz)
                act(B["zs"], Xz, AF.Silu)
                Xn = [p1_bank(), p1_bank()]
                N2 = [p1_nset(), p1_nset()]
                for s in range(2):
                    tt("pool", N2[s]["sqb"], outs[s], outs[s], ALU.mult)
                for s in range(2):
                    mm(Xn[s], ones_b, N2[s]["sqb"])
                for s in range(2):
                    act(N2[s]["srr"], Xn[s], AF.Ln, bias=EPS, scale=1.0)
                for s in range(2):
                    act(N2[s]["rs"], N2[s]["srr"], AF.Exp, scale=-0.5)
                stt(B["QnT"], B["QnT"], float(128 ** -0.5), N2[0]["rs"], ALU.mult, ALU.mult)
                tt("dve", B["KnT"], B["KnT"], N2[1]["rs"], ALU.mult)

            def gdn_head(B, h):
                Sh = Sst.c(h * 128, (h + 1) * 128)
                Sbh = Sbf.c(h * 128, (h + 1) * 128)
                X = B["X"]
                Xb = V(X.ap.bitcast(BF16), X.mem, 0, 2048, BF16, True)
                R = [X.c(i * 128, (i + 1) * 128) for i in range(4)]
                for t in range(4):
                    col = t * 8 + h
                    KnTt = B["KnT"].c(t * 128, (t + 1) * 128)
                    QnTt = B["QnT"].c(t * 128, (t + 1) * 128)
                    VTt = B["VT"].c(t * 128, (t + 1) * 128)
                    pK = Xb.c(768, 896)
                    pV = Xb.c(896, 1024)
                    op("pe", lambda e: e.transpose(pK.ap, KnTt.ap, ident_b.ap), [KnTt, ident_b], [pK])
                    op("pe", lambda e: e.transpose(pV.ap, VTt.ap, ident_b.ap), [VTt, ident_b], [pV])
                    yield
                    kg, kd, vt = B["Kg"], B["Kdec"], B["Vt"]
                    act(kg, pK, AF.Copy, scale=eg.c(col, col + 1))
                    if GDN_BAL >= 1:
                        act(kd, pK, AF.Copy, scale=egl.c(col, col + 1))
                    else:
                        ts("dve", kd, pK, egl.c(col, col + 1), None, ALU.mult)
                    cp("act", vt, pV)
                    Gm = B["Gm"]
                    ts("dve", Gm, su_f, gv.c(col, col + 1), None, ALU.mult)
                    yield
                    pKK, pQK, pD = R[0], R[1], R[2]
                    mm(pKK, KnTt, KnTt)
                    mm(pQK, KnTt, QnTt)
                    mm(pD, Gm, tri_f)
                    yield
                    decT, dinc, dstr = B["decT"], B["dinc"], B["dstr"]
                    act(decT, pD, AF.Exp)
                    yield
                    tt("pool", dinc, decT, mincl_f, ALU.mult)
                    tt("pool", dstr, dinc, mstr_f, ALU.mult)
                    yield
                    at = B["attnT"]
                    W, WT, Y = B["W"][0], B["WT"][0], B["Y"][0]
                    stt(W, pKK, nbeta.c(col, col + 1), dstr, ALU.mult, ALU.mult)
                    tt("dve", at, pQK, dinc, ALU.mult)
                    yield
                    pT = R[0]
                    op("pe", lambda e: e.transpose(pT.ap, W.ap, ident_f.ap), [W, ident_f], [pT])
                    tt("pool", Y, W, ident_f, ALU.add)
                    yield
                    cp("act", WT, pT)
                    yield
                    cur = 0
                    pA, pB, pC = R[1], R[2], R[3]
                    for k in range(1, 7):
                        nxt = 1 - cur
                        Wc, WTc, Yc = B["W"][cur], B["WT"][cur], B["Y"][cur]
                        Wn, WTn, Yn = B["W"][nxt], B["WT"][nxt], B["Y"][nxt]
                        mm(pB, Wc, WTc)
                        if k <= 5:
                            mm(pA, WTc, Wc)
                        yield
                        cp("act" if (GDN_BAL >= 2 and k % 2 == 0) else "dve", WTn, pB)
                        if k <= 5:
                            cp("act", Wn, pA)
                        yield
                        mm(pC, WTn, Yc)
                        yield
                        tt("dve", Yn, pC, Yc, ALU.add)
                        yield
                        cur = nxt
                    Ybf = B["Ybf"]
                    cp("act", Ybf, B["Y"][cur])
                    yield
                    pu0, pw0 = R[0], R[1]
                    mm(pu0, Ybf, vt)
                    mm(pw0, kg, Ybf)
                    yield
                    ub_, wt_ = B["u0b"], B["w0T"]
                    act(ub_, pu0, AF.Copy, scale=beta.c(col, col + 1))
                    cp("act" if GDN_BAL >= 1 else "dve", wt_, pw0)
                    yield
                    pwS, pQS = R[2], R[3]
                    mm(pwS, wt_, Sbh)
                    mm(pQS, QnTt, Sbh)
                    yield
                    vnew = B["vnew"]
                    stt(vnew, pwS, nbeta.c(col, col + 1), ub_, ALU.mult, ALU.add)
                    o1 = B["o1"]
                    act(o1, pQS, AF.Copy, scale=eg.c(col, col + 1))
                    yield
                    pAV, pKV = R[0], R[1]
                    mm(pAV, at, vnew)
                    mm(pKV, kd, vnew)
                    yield
                    oo = B["oo"]
                    tt("dve", oo, pAV, o1, ALU.add)
                    stt(Sh, Sh, egt.c(col, col + 1), pKV, ALU.mult, ALU.add)
                    yield
                    cp("act", Sbh, Sh)
                    ms, msr, rst, on = B["ms"], B["msr"], B["rst"], B["on"]
                    junk = B["junk"]
                    op("dve", lambda e: e.scalar_tensor_tensor(junk.ap, oo.ap, 1.0, oo.ap, ALU.mult, ALU.mult, accum_out=ms.ap),
                       [oo], [junk, ms])
                    yield
                    ts("pool", msr, ms, 1.0 / 128, EPS, ALU.mult, ALU.add)
                    tt("pool", rst, msr, neghalf, ALU.pow)
                    yield
                    act(on, oo, AF.Copy, scale=rst)
                    yield
                    pO = Xb.c(512, 640)
                    op("pe", lambda e: e.transpose(pO.ap, on.ap, ident_b.ap), [on, ident_b], [pO])
                    yield
                    stt(oTn[h].c(t * 128, (t + 1) * 128), pO, vcol("gdn_norm"), B["zs"].c(t * 128, (t + 1) * 128), ALU.mult, ALU.mult)
                    yield

            gdn_phase1(slots[0], 0)
            for s in range(NI):
                if s + 1 < NI:
                    gdn_phase1(slots[s + 1], s + 1)
                gdn_phase1b(slots[s], s)
            gens = [gdn_head(slots[s], s) for s in range(NI)]
            for s in range(NI):
                for _ in range(GDN_STAGGER[s % len(GDN_STAGGER)]):
                    next(gens[s])
            while gens:
                for gg in list(gens):
                    try:
                        next(gg)
                    except StopIteration:
                        gens.remove(gg)
            AR.top = top1
            if g == 0 and lsplit < ltot:
                convert(lsplit, ltot)
            oBT = [AR.alloc(TG, BF16) for _ in range(NH)]
            top1b = AR.top
            QnopeT = [AR.alloc(TG, BF16) for _ in range(NH)]
            QropeT = [AR.alloc(TG, BF16) for _ in range(NH)]
            NCH = 4
            kch = [AR.alloc(TG, BF16) for _ in range(NCH)]
            vch = [AR.alloc(TG, BF16) for _ in range(NCH)]
            pbuf = [AR.alloc(TG, BF16) for _ in range(3)]
            rl = AR.alloc(TG, F32)
            top2 = AR.top
            qd = [AR.alloc(TG, F32) for _ in range(3)]
            qdn = [AR.alloc(TG, BF16) for _ in range(3)]
            ckv = [AR.alloc(TG, F32) for _ in range(2)]
            ckvn = [AR.alloc(TG, BF16) for _ in range(2)]
            for i in range(3):
                u = load_unit(f"in_qd{i}")
                proj_chunk(u, 8, uTc, ps[i % 2])
                cp("act", qd[i], ps[i % 2])
            for i in range(2):
                u = load_unit(f"in_ckv{i}")
                proj_chunk(u, 8, uTc, ps[2 + i])
                cp("act", ckv[i], ps[2 + i])
            topn = AR.top
            rmsnorm_T(qd, QL, "q_a_norm", qdn, 0)
            AR.top = topn
            rmsnorm_T(ckv, KVL, "kv_a_norm", ckvn, 0)
            AR.top = topn
            posi = AR.alloc(TG, I32)
            ang = AR.alloc(TG, F32)
            tq = AR.alloc(TG, F32)
            ni = AR.alloc(TG, I32)
            nf = AR.alloc(TG, F32)
            rr_ = AR.alloc(TG, F32)
            dd = AR.alloc(TG, F32)
            dma("sp", P64(posi), V(pos_d[:, t0:t0 + TG], misc_m, 4, 5, I32))
            cp("dve", P64(tq), P64(posi))
            ts("dve", P64(ang), P64(tq), vcol("invfreq").p(0, 64), None, ALU.mult)
            C1 = 6.28125
            C2 = float(2 * np.pi - 6.28125)
            a_ = P64(ang)
            ts("dve", P64(tq), a_, float(1.0 / (2 * np.pi)), None, ALU.mult)
            cp("dve", P64(ni), P64(tq))
            cp("dve", P64(nf), P64(ni))
            stt(P64(rr_), P64(nf), -C1, a_, ALU.mult, ALU.add)
            stt(P64(rr_), P64(nf), -C2, P64(rr_), ALU.mult, ALU.add)
            for which in range(2):
                if which == 1:
                    ts("dve", P64(rr_), P64(rr_), PI / 2, None, ALU.add)
                ts("dve", P64(dd), P64(rr_), PI, float(-2 * np.pi), ALU.is_gt, ALU.mult)
                tt("dve", P64(rr_), P64(rr_), P64(dd), ALU.add)
                ts("dve", P64(dd), P64(rr_), -PI, float(2 * np.pi), ALU.is_lt, ALU.mult)
                tt("dve", P64(rr_), P64(rr_), P64(dd), ALU.add)
                ts("dve", P64(rr_), P64(rr_), PI, -PI, ALU.min, ALU.max)
                if which == 0:
                    act(P64(tq), P64(rr_), AF.Sin)
                    ts("dve", s2, P64(tq), vcol("sgn").p(0, 64), None, ALU.mult)
                else:
                    act(c2, P64(rr_), AF.Sin)
            AR.top = topn
            t1 = AR.alloc(TG, F32)
            t2 = AR.alloc(TG, F32)
            u = load_unit("in_kpe")
            P1 = ps[0]
            P2 = ps[1]
            proj_chunk(u, 8, uTc, P1, mcols=64, ucol0=0, ustride=128)
            proj_chunk(u, 8, uTc, P2, mcols=64, ucol0=64, ustride=128)
            tt("dve", P64(t1), P1.p(0, 64), c2, ALU.mult)
            tt("dve", P64(t2), P2.p(0, 64), s2, ALU.mult)
            tt("pool", kpe.c(t0, t0 + TG), P64(t1), P64(t2), ALU.add)
            Kst = [AR.alloc(TG, BF16) for _ in range(2)]
            uk = load_unit("kv_k")
            for h in range(NH):
                P = ps[2 + (h % 2)]
                for kc in range(2):
                    mm(P, uk.c(kc * 1024 + h * 128, kc * 1024 + h * 128 + 128), ckvn[kc], kc == 0, kc == 1)
                ks = Kst[h % 2]
                cp("act" if h % 2 == 0 else "dve", ks, P)
                dma("act", V(kc_d[h, :, t0:t0 + TG], kc_m[h], t0, t0 + TG, BF16), ks)
            Vst = [AR.alloc(TG, BF16) for _ in range(2)]
            uv = load_unit("kv_v")
            for t in range(4):
                tile_i = g * 4 + t
                for half in range(2):
                    P = ps[2 + half]
                    for kc in range(2):
                        mm(P, ckvn[kc].c(t * 128, (t + 1) * 128), uv.c(kc * 1024 + half * 512, kc * 1024 + half * 512 + 512), kc == 0, kc == 1)
                    vs = Vst[half]
                    cp("act" if half == 0 else "dve", vs, P)
                    for j in range(4):
                        hh = half * 4 + j
                        dma("act", V(vc_d[hh, :, tile_i, :], vc_m[hh], tile_i, tile_i + 1, BF16), vs.c(j * 128, (j + 1) * 128))
            def q_proj(h):
                u = load_unit(f"q_up{h}")
                b0 = 3 * (h % 2)
                proj_chunk(u, 3, qdn, ps[b0], mcols=128, ucol0=0, ustride=256)
                proj_chunk(u, 3, qdn, ps[b0 + 1], mcols=64, ucol0=128, ustride=256)
                proj_chunk(u, 3, qdn, ps[b0 + 2], mcols=64, ucol0=192, ustride=256)

            def q_evac(h):
                b0 = 3 * (h % 2)
                ta, tb = t12[h % 2]
                act(QnopeT[h], ps[b0], AF.Copy, scale=SCALE)
                stt(P64(ta), ps[b0 + 1].p(0, 64), SCALE, c2, ALU.mult, ALU.mult)
                stt(P64(tb), ps[b0 + 2].p(0, 64), SCALE, s2, ALU.mult, ALU.mult)
                tt("pool", P64(QropeT[h]), P64(ta), P64(tb), ALU.add)

            t12 = [(t1, t2), (AR.alloc(TG, F32), AR.alloc(TG, F32))]
            q_proj(0)
            for h in range(NH):
                if h + 1 < NH:
                    q_proj(h + 1)
                q_evac(h)
            AR.top = top2

            nkt = (g + 1) * 4
            pairs = [(h, kt) for h in range(NH) for kt in range(nkt)]
            chunk_of = {}
            nchunk = [0]

            def a_qk(i):
                h, kt = pairs[i]
                if kt % 4 == 0:
                    ci = nchunk[0] % NCH
                    nchunk[0] += 1
                    chunk_of[(h, kt // 4)] = ci
                    k0 = kt * 128
                    dma("sp", kch[ci], V(kc_d[h, :, k0:k0 + TG], kc_m[h], k0, k0 + TG, BF16))
                    dma("sp", vch[ci], V(vc_d[h, :, kt:kt + 4, :], vc_m[h], kt, kt + 4, BF16),
                        out_ap=vch[ci].ap.rearrange("p (t d) -> p t d", d=128))
                ci = chunk_of[(h, kt // 4)]
                kk = kt % 4
                j = kt - g * 4
                q0 = max(0, j) * 128
                SB = ps[i % 3].c(q0, TG)
                mm(SB, kch[ci].c(kk * 128, (kk + 1) * 128), QnopeT[h].c(q0, TG), True, False)
                mm(SB, kpe.c(kt * 128, (kt + 1) * 128), P64(QropeT[h]).c(q0, TG), False, j < 0)
                if j >= 0:
                    mm(ps[i % 3].c(q0, q0 + 128), ident_b, negm_b, False, True)

            def a_pv(i):
                h, kt = pairs[i]
                ci = chunk_of[(h, kt // 4)]
                kk = kt % 4
                OB, LB = (ps[4], ps[5]) if h % 2 == 0 else (ps[6], ps[7])
                j = kt - g * 4
                q0 = max(0, j) * 128
                SB = ps[i % 3].c(q0, TG)
                pb_ = pbuf[i % 3].c(q0, TG)
                act(pb_, SB, AF.Exp)
                mm(OB.c(q0, TG), vch[ci].c(kk * 128, (kk + 1) * 128), pb_, kt == 0, kt == nkt - 1)
                mm(LB.c(q0, TG), ones_b, pb_, kt == 0, kt == nkt - 1)
                if kt == nkt - 1:
                    op("dve", lambda e: e.reciprocal(rl.ap, LB.ap), [LB], [rl])
                    tt("dve", oBT[h], OB, rl, ALU.mult)

            def attention_gen():
                a_qk(0)
                for i in range(len(pairs)):
                    if i + 1 < len(pairs):
                        a_qk(i + 1)
                    a_pv(i)
                    yield

            att = [attention_gen()]

            def att_step(n=1):
                for _ in range(n):
                    if att[0] is None:
                        return
                    try:
                        next(att[0])
                    except StopIteration:
                        att[0] = None

            att_step(100000)
            AR.top = top1b
            sgA = [AR.alloc(TG, F32) for _ in range(2)]
            mA = [AR.alloc(TG, F32) for _ in range(2)]
            mB = [AR.alloc(TG, F32) for _ in range(2)]
            mT = [AR.alloc(TG, BF16) for _ in range(8)]
            for c in range(8):
                ua = load_unit(f"proj_a{c}")
                Pa = ps[0]
                for h in range(NH):
                    mm(Pa, ua.c(h * 128, (h + 1) * 128), oTn[h], h == 0, h == NH - 1)
                ug = load_unit(f"in_ga{c}")
                Pg = ps[1]
                proj_chunk(ug, 8, uTc, Pg)
                if g == 0:
                    cp("act", sgA[0], Pa)
                    dbg_store("yA", sgA[0], rows=(c * 128, (c + 1) * 128))
                act(sgA[c % 2], Pg, AF.Sigmoid)
                tt("dve", mA[c % 2], Pa, sgA[c % 2], ALU.mult)
                ub2 = load_unit(f"proj_b{c}")
                Pb = ps[2]
                for h in range(NH):
                    mm(Pb, ub2.c(h * 128, (h + 1) * 128), oBT[h], h == 0, h == NH - 1)
                ug2 = load_unit(f"in_gb{c}")
                Pg2 = ps[3]
                proj_chunk(ug2, 8, uTc, Pg2)
                if g == 0:
                    cp("act", mB[0], Pb)
                    dbg_store("yB", mB[0], rows=(c * 128, (c + 1) * 128))
                act(sgA[(c + 1) % 2], Pg2, AF.Sigmoid)
                tt("dve", mB[c % 2], Pb, sgA[(c + 1) % 2], ALU.mult)
                tt("pool", mT[c], mA[c % 2], mB[c % 2], ALU.add)
            for c in range(8):
                uo = load_unit(f"w_o{c}")
                Po = ps[4] if c % 2 == 0 else ps[6]
                for k in range(8):
                    mm(Po, uo.c(k * 128, (k + 1) * 128), mT[k], k == 0, k == 7)
                tt("dve", hTc[c], Po, hTc[c], ALU.add)
                if c >= 1:
                    norm_acc_pe(c - 1)
                norm_acc_sq(c)
            norm_acc_pe(7)
            AR.top = top0

        for g in range(ngroups):
            t0 = g * TG
            topA = AR.top
            xin = [AR.alloc(D, F32) for _ in range(2)]
            for t in range(4):
                xi = xin[t % 2]
                r0 = t0 + t * 128
                dma("sp", xi, V(x_d[r0:r0 + 128, :], x_m, r0, r0 + 128, F32))
                for half in range(2):
                    pb = ps[6 + half]
                    for j in range(4):
                        c = half * 4 + j
                        src = xi.c(c * 128, (c + 1) * 128)
                        dst = pb.c(j * 128, (j + 1) * 128)
                        op("pe", lambda e, src=src, dst=dst: e.transpose(dst.ap, src.ap, ident_f.ap), [src, ident_f], [dst])
                    o_ap = hT.ap[:, half * 4 * TG:(half * 4 + 4) * TG].rearrange("p (c t) -> p c t", c=4)[:, :, t * 128:(t + 1) * 128]
                    i_ap = pb.ap.rearrange("p (c t) -> p c t", c=4)
                    hv = V(o_ap, hT.mem, half * 4 * TG * 4, (half * 4 + 4) * TG * 4, F32)
                    eng = "act" if half == 0 else "dve"
                    if eng == "act":
                        op("act", lambda e, o_ap=o_ap, i_ap=i_ap: e.copy(o_ap, i_ap), [pb], [hv])
                    else:
                        op("dve", lambda e, o_ap=o_ap, i_ap=i_ap: e.tensor_copy(o_ap, i_ap), [pb], [hv])
            AR.top = topA
            ffn("ffn1", "ffn1_norm")
            if g == 0:
                for c in range(8):
                    dbg_store("h1", hTc[c], rows=(c * 128, (c + 1) * 128))
            if MIXER_ENABLED:
                mixer(g, t0)
            ffn("ffn2", "ffn2_norm", pre_acc=MIXER_ENABLED)
            top = AR.top
            yT = [AR.alloc(TG, F32) for _ in range(8)]
            sr = AR.alloc(TG, F32)
            rstd = AR.alloc(TG, F32)
            acc = ps[5]
            op("act", lambda e: e.activation(out=sr.ap, in_=acc.ap, func=AF.Ln, bias=EPS, scale=1.0 / D), [acc], [sr])
            op("act", lambda e: e.activation(out=rstd.ap, in_=sr.ap, func=AF.Exp, scale=-0.5), [sr], [rstd])
            for c in range(8):
                gcol = vcol("final_norm", c)
                op("dve", lambda e, c=c, gcol=gcol: e.scalar_tensor_tensor(yT[c].ap, hTc[c].ap, gcol.ap, rstd.ap, ALU.mult, ALU.mult),
                   [hTc[c], vec, rstd], [yT[c]])
            ot = [AR.alloc(D, F32) for _ in range(2)]
            for t in range(4):
                o = ot[t % 2]
                for half in range(2):
                    pb = ps[6 + half]
                    for j in range(4):
                        c = half * 4 + j
                        src = yT[c].c(t * 128, (t + 1) * 128)
                        dst = pb.c(j * 128, (j + 1) * 128)
                        op("pe", lambda e, src=src, dst=dst: e.transpose(dst.ap, src.ap, ident_f.ap), [src, ident_f], [dst])
                    od = o.c(half * 512, (half + 1) * 512)
                    if half == 0:
                        op("act", lambda e, od=od, pb=pb: e.copy(od.ap, pb.ap), [pb], [od])
                    else:
                        op("dve", lambda e, od=od, pb=pb: e.tensor_copy(od.ap, pb.ap), [pb], [od])
                r0 = t0 + t * 128
                dma("pool", V(out_d[r0:r0 + 128, :], out_m, r0, r0 + 128, F32), o)
            AR.top = top

        deps = {}
        out_m.deps(0, T, True, deps)
        misc_m.deps(0, 16, True, deps)
        kb.wait_all("pool", list(deps.items()))
    return nc


_CACHE = {}


def kernel(**inputs):
    inp = {k: np.asarray(v) for k, v in inputs.items()}
    lay = build_layout(inp)
    w32 = np.ascontiguousarray(np.concatenate(lay.parts, axis=1))
    vecs = build_vecs(inp)
    consts = build_consts()
    key = "prog"
    if key not in _CACHE:
        _CACHE[key] = build_program(lay.off, lay.n, vecs.shape[1], lsplit=lay.split)
    nc = _CACHE[key]
    x = inp["x"]
    pos = inp["positions"]
    in_maps = []
    for b in range(8):
        in_maps.append({
            "x": np.ascontiguousarray(x[b]),
            "pos": np.ascontiguousarray(np.broadcast_to(pos[b][None, :], (64, T))).astype(np.int32),
            "w32": w32, "vecs": vecs, "consts": consts,
        })
    res = run_bass_kernel_spmd(nc, in_maps, core_ids=list(range(8)))
    out = np.stack([np.asarray(r["out"]) for r in res.results], axis=0)
    return out.astype(np.float32)
```

```python
import numpy as np
from contextlib import ExitStack
import concourse.bass as bass
import concourse.mybir as mybir
from concourse.bass_utils import run_bass_kernel_spmd

F32 = mybir.dt.float32
BF16 = mybir.dt.bfloat16
I32 = mybir.dt.int32
AF = mybir.ActivationFunctionType
ALU = mybir.AluOpType

D = 1024
T = 4096
DFF = 2816
NF = DFF // 128
TG = 512
NG = T // TG
EPS = 1e-6
NH = 8
QL = 384
KVL = 256
SCALE = float((128 + 64) ** -0.5)
PI = float(np.pi)
OWN_SKIP = ("pe",)
GDN_P1POOL = 1
GDN_BAL = 1
GDN_STAGGER = (7, 6, 5, 4, 3, 2, 1, 0)
ATT_P1 = 4
ATT_RR = 1
MIXER_ENABLED = True


def _unit_cols(w, cols):
    k = w.shape[0]
    sub = w[:, cols]
    return sub.reshape(k // 128, 128, len(cols)).transpose(1, 0, 2).reshape(128, -1)


class Layout:
    def __init__(self):
        self.off = {}
        self.parts = []
        self.n = 0

    def add(self, name, arr):
        arr = np.ascontiguousarray(arr, dtype=np.float32)
        assert arr.shape[0] == 128
        self.off[name] = (self.n, arr.shape[1])
        self.parts.append(arr)
        self.n += arr.shape[1]


def build_layout(inp, shapes_only=False):
    L = Layout()

    def W(name):
        a = inp[name]
        return a[0]

    def add_ffn(pre):
        wg, wu, wd = W(pre + "_w_gate"), W(pre + "_w_up"), W(pre + "_w_down")
        for f in range(NF):
            cols = np.arange(f * 128, (f + 1) * 128)
            g = _unit_cols(wg, cols).reshape(128, 8, 1, 128)
            u = _unit_cols(wu, cols).reshape(128, 8, 1, 128)
            L.add(f"{pre}_gu{f}", np.concatenate([g, u], axis=2).reshape(128, -1))
        for c in range(8):
            cols = np.arange(c * 128, (c + 1) * 128)
            L.add(f"{pre}_d{c}", _unit_cols(wd, cols))

    add_ffn("ffn1")
    win = W("w_in")
    names = ["q", "k", "v", "z"]
    for s in range(4):
        for h in range(8):
            c0 = s * 1024 + h * 128
            L.add(f"in_{names[s]}{h}", _unit_cols(win, np.arange(c0, c0 + 128)))
    L.add("in_bg", _unit_cols(win, np.arange(4096, 4112)))
    for i in range(3):
        L.add(f"in_qd{i}", _unit_cols(win, np.arange(4112 + i * 128, 4112 + (i + 1) * 128)))
    for i in range(2):
        L.add(f"in_ckv{i}", _unit_cols(win, np.arange(4496 + i * 128, 4496 + (i + 1) * 128)))
    kp = np.arange(4752, 4816)
    kps = np.concatenate([kp[32:], kp[:32]])
    L.add("in_kpe", np.concatenate([_unit_cols(win, kp).reshape(128, 8, 64),
                                    _unit_cols(win, kps).reshape(128, 8, 64)], axis=2).reshape(128, -1))
    for i in range(8):
        L.add(f"in_ga{i}", _unit_cols(win, np.arange(4816 + i * 128, 4816 + (i + 1) * 128)))
    for i in range(8):
        L.add(f"in_gb{i}", _unit_cols(win, np.arange(5840 + i * 128, 5840 + (i + 1) * 128)))
    wq = W("w_q_up")
    for h in range(8):
        b = h * 192
        nope = np.arange(b, b + 128)
        r = np.arange(b + 128, b + 192)
        rs = np.concatenate([r[32:], r[:32]])
        L.add(f"q_up{h}", np.concatenate([_unit_cols(wq, nope).reshape(128, 3, 128),
                                          _unit_cols(wq, r).reshape(128, 3, 64),
                                          _unit_cols(wq, rs).reshape(128, 3, 64)], axis=2).reshape(128, -1))
    wkv = W("w_kv_up")
    kcols = np.concatenate([np.arange(h * 256, h * 256 + 128) for h in range(8)])
    vcols = np.concatenate([np.arange(h * 256 + 128, h * 256 + 256) for h in range(8)])
    L.add("kv_k", _unit_cols(wkv, kcols))
    L.add("kv_v", _unit_cols(wkv, vcols))
    for nm in ("proj_a", "proj_b", "w_o"):
        w = W(nm)
        for c in range(8):
            L.add(f"{nm}{c}", _unit_cols(w, np.arange(c * 128, (c + 1) * 128)))
    L.split = L.n
    add_ffn("ffn2")
    return L


VEC = {}


def build_vecs(inp):
    cols = []

    def add(name, a):
        a = np.asarray(a, np.float32)
        if a.ndim == 1:
            a = a[:, None]
        VEC[name] = (sum(c.shape[1] for c in cols), a.shape[1])
        cols.append(a)

    def chunks(v):
        return v.reshape(-1, 128).T

    add("ffn1_norm", chunks(inp["ffn1_norm"][0]))
    add("mix_norm", chunks(inp["mix_norm"][0]))
    add("ffn2_norm", chunks(inp["ffn2_norm"][0]))
    add("final_norm", chunks(inp["final_norm"]))
    cw = inp["conv_w"][0]
    add("conv", np.concatenate([chunks(cw[j]) for j in range(4)], axis=1))
    add("gdn_norm", inp["gdn_norm"][0])
    add("q_a_norm", chunks(inp["q_a_norm"][0]))
    add("kv_a_norm", chunks(inp["kv_a_norm"][0]))
    add("a_log", np.broadcast_to(inp["a_log"][0][None, :], (128, 8)))
    add("dt_bias", np.broadcast_to(inp["dt_bias"][0][None, :], (128, 8)))
    inv = (10000.0 ** (-np.arange(0, 64, 2, dtype=np.float32) / 64)).astype(np.float32)
    f2 = np.zeros(128, np.float32)
    f2[:64] = np.concatenate([inv, inv])
    add("invfreq", f2)
    sg = np.zeros(128, np.float32)
    sg[:32] = -1.0
    sg[32:64] = 1.0
    add("sgn", sg)
    return np.ascontiguousarray(np.concatenate(cols, axis=1))


def build_consts():
    i = np.arange(128)
    ident = np.eye(128, dtype=np.float32)
    ones = np.ones((128, 128), np.float32)
    tri = (i[:, None] <= i[None, :]).astype(np.float32)
    su = (i[:, None] > i[None, :]).astype(np.float32)
    mincl = (i[None, :] >= i[:, None]).astype(np.float32)
    mstr = (i[None, :] > i[:, None]).astype(np.float32)
    negm = np.where(i[:, None] > i[None, :], -30000.0, 0.0).astype(np.float32)
    return np.ascontiguousarray(np.concatenate([ident, ones, tri, su, mincl, mstr, negm], axis=1))


class Mem:
    def __init__(self, size):
        self.recs = [[0, size, None, {}]]

    def deps(self, lo, hi, write, out):
        for r in self.recs:
            if r[0] < hi and lo < r[1]:
                if r[2] is not None:
                    s, v = r[2]
                    if out.get(s, 0) < v:
                        out[s] = v
                if write:
                    for s, v in r[3].items():
                        if out.get(s, 0) < v:
                            out[s] = v

    def commit(self, lo, hi, write, dep):
        new = []
        for r in self.recs:
            if r[0] < hi and lo < r[1]:
                if r[0] < lo:
                    new.append([r[0], lo, r[2], dict(r[3])])
                if r[1] > hi:
                    new.append([hi, r[1], r[2], dict(r[3])])
                if not write:
                    rd = dict(r[3])
                    if rd.get(dep[0], 0) < dep[1]:
                        rd[dep[0]] = dep[1]
                    new.append([max(r[0], lo), min(r[1], hi), r[2], rd])
            else:
                new.append(r)
        if write:
            new.append([lo, hi, dep, {}])
        self.recs = new


ES = {F32: 4, BF16: 2, I32: 4}


class V:
    def __init__(self, ap, mem, lo, hi, dt, whole=False):
        self.ap, self.mem, self.lo, self.hi, self.dt, self.whole = ap, mem, lo, hi, dt, whole

    def c(self, a, b):
        e = ES[self.dt]
        if self.whole:
            return V(self.ap[:, a:b], self.mem, self.lo, self.hi, self.dt, True)
        return V(self.ap[:, a:b], self.mem, self.lo + a * e, self.lo + b * e, self.dt)

    def p(self, a, b):
        return V(self.ap[a:b], self.mem, self.lo, self.hi, self.dt, self.whole)

    def pc(self, p0, p1, a, b):
        e = ES[self.dt]
        if self.whole:
            return V(self.ap[p0:p1, a:b], self.mem, self.lo, self.hi, self.dt, True)
        return V(self.ap[p0:p1, a:b], self.mem, self.lo + a * e, self.lo + b * e, self.dt)


class Eng:
    def __init__(self, h, sem, sid):
        self.h, self.sem, self.sid = h, sem, sid
        self.count = 0
        self.known = {}
        self.dsems = []
        self.dma_i = 0


class KB:
    def __init__(self, nc, es):
        self.nc = nc
        self.es = es
        self.sems = {}
        self.eng = {}
        for name, h in (("pe", nc.tensor), ("act", nc.scalar), ("dve", nc.vector),
                        ("pool", nc.gpsimd), ("sp", nc.sync)):
            sem = es.enter_context(nc.semaphore("s_" + name))
            self.sems[name] = sem
            self.eng[name] = Eng(h, sem, name)
        for q in ("sp", "pool", "act"):
            for i in range(8):
                sid = f"d_{q}{i}"
                sem = es.enter_context(nc.semaphore(sid))
                self.sems[sid] = sem
                self.eng[q].dsems.append(sid)
        self.nins = 0

    def sb(self, name, cols, dt, parts=128):
        t = self.es.enter_context(self.nc.sbuf_tensor(name, [parts, cols], dt))
        return V(t[:], Mem(cols * ES[dt]), 0, cols * ES[dt], dt)

    def _waits(self, E, reads, writes, extra=None):
        deps = {}
        for v in reads:
            v.mem.deps(v.lo, v.hi, v.whole, deps)
        for v in writes:
            v.mem.deps(v.lo, v.hi, True, deps)
        if extra:
            for s, val in extra.items():
                if deps.get(s, 0) < val:
                    deps[s] = val
        for s, val in deps.items():
            if s == E.sid and E.sid in OWN_SKIP:
                continue
            if E.known.get(s, 0) < val:
                E.h.wait_ge(self.sems[s], val)
                E.known[s] = val
                self.nins += 1

    def op(self, eng, fn, reads, writes):
        E = self.eng[eng]
        self._waits(E, reads, writes)
        ins = fn(E.h)
        E.count += 1
        ins.then_inc(E.sem, 1)
        dep = (E.sid, E.count)
        for v in reads:
            v.mem.commit(v.lo, v.hi, v.whole, dep)
        for v in writes:
            v.mem.commit(v.lo, v.hi, True, dep)
        self.nins += 1

    def dma(self, q, out_v, in_v, out_ap=None, in_ap=None, **kw):
        E = self.eng[q]
        R = len(E.dsems)
        sid = E.dsems[E.dma_i % R]
        uses = E.dma_i // R
        E.dma_i += 1
        self._waits(E, [in_v], [out_v], extra={sid: 16 * uses} if uses else None)
        ins = E.h.dma_start(out=out_ap if out_ap is not None else out_v.ap,
                            in_=in_ap if in_ap is not None else in_v.ap, **kw)
        ins.then_inc(self.sems[sid], 16)
        dep = (sid, 16 * (uses + 1))
        in_v.mem.commit(in_v.lo, in_v.hi, False, dep)
        out_v.mem.commit(out_v.lo, out_v.hi, True, dep)
        self.nins += 1
        return dep

    def wait_all(self, q, deps):
        E = self.eng[q]
        for s, val in deps:
            if E.known.get(s, 0) < val:
                E.h.wait_ge(self.sems[s], val)
                E.known[s] = val


class Arena:
    def __init__(self, kb, name, nbytes):
        self.t = kb.es.enter_context(kb.nc.sbuf_tensor(name, [128, nbytes // 2], BF16))
        self.mem = Mem(nbytes)
        self.top = 0
        self.nbytes = nbytes

    def alloc(self, cols, dt):
        nb = cols * ES[dt]
        nb_al = (nb + 63) // 64 * 64
        lo = self.top
        assert lo + nb_al <= self.nbytes, ("arena overflow", lo, nb_al, self.nbytes)
        self.top += nb_al
        ap = self.t[:, lo // 2:(lo + nb) // 2]
        if dt != BF16:
            ap = ap.bitcast(dt)
        return V(ap, self.mem, lo, lo + nb, dt)


def build_program(lay_off, ltot, nvec, dbg=None, ngroups=NG, stop=99, lsplit=None):
    lsplit = ltot if lsplit is None else lsplit
    nc = bass.Bass("TRN2", target_bir_lowering=False)
    x_d = nc.dram_tensor("x", [T, D], F32, kind="ExternalInput").ap()
    pos_d = nc.dram_tensor("pos", [64, T], I32, kind="ExternalInput").ap()
    w32_d = nc.dram_tensor("w32", [128, ltot], F32, kind="ExternalInput").ap()
    vec_d = nc.dram_tensor("vecs", [128, nvec], F32, kind="ExternalInput").ap()
    cst_d = nc.dram_tensor("consts", [128, 7 * 128], F32, kind="ExternalInput").ap()
    out_d = nc.dram_tensor("out", [T, D], F32, kind="ExternalOutput").ap()
    wbf_d = nc.dram_tensor("wbf", [128, ltot], BF16).ap()
    kc_d = nc.dram_tensor("kcache", [NH, 128, T], BF16).ap()
    vc_d = nc.dram_tensor("vcache", [NH, 128, T // 128, 128], BF16).ap()
    dbg_d = {}
    if dbg:
        for nm, shp in dbg.items():
            dbg_d[nm] = nc.dram_tensor("dbg_" + nm, list(shp), F32, kind="ExternalOutput").ap()

    with ExitStack() as es:
        kb = KB(nc, es)
        op, dma = kb.op, kb.dma
        w32_v = V(w32_d, Mem(ltot), 0, ltot, F32)
        wbf_m = Mem(ltot)
        x_m = Mem(T)
        out_m = Mem(T)
        kc_m = [Mem(T) for _ in range(NH)]
        vc_m = [Mem(T // 128) for _ in range(NH)]
        misc_m = Mem(16)

        def dv(ap, mem, lo, hi, dt=F32):
            return V(ap, mem, lo, hi, dt)

        cst = kb.sb("cst", 7 * 128, F32)
        cstb = kb.sb("cstb", 3 * 128, BF16)
        vec = kb.sb("vec", nvec, F32)
        hT = kb.sb("hT", 8 * TG, F32)
        uT = kb.sb("uT", 8 * TG, BF16)
        RS = 2816
        NSLOT = 5
        ring = [kb.sb(f"ring{i}", RS, BF16) for i in range(NSLOT)]
        halo = kb.sb("halo", 24 * 3, F32)
        Sst = kb.sb("Sst", NH * 128, F32)
        Sbf = kb.sb("Sbf", NH * 128, BF16)
        kpe = kb.sb("kpe", T, BF16, parts=64)
        c2 = kb.sb("c2", TG, F32, parts=64)
        s2 = kb.sb("s2", TG, F32, parts=64)
        nega = kb.sb("nega", 8, F32)
        neghalf = kb.sb("neghalf", 1, F32)
        lnqs = kb.sb("lnqs", 1, F32)
        AR = Arena(kb, "arena", 128 * 1024)
        nsq = [kb.sb("nsq0", TG, BF16), kb.sb("nsq1", TG, BF16)]
        ps = []
        for i in range(8):
            t = es.enter_context(nc.psum_tensor(f"ps{i}", [128, 512], F32))
            ps.append(V(t[:], Mem(2048), 0, 2048, F32, True))

        def psb(i):
            v = ps[i]
            return V(v.ap.bitcast(BF16), v.mem, 0, 2048, BF16, True)

        ident_f = cst.c(0, 128)
        ones_f = cst.c(128, 256)
        tri_f = cst.c(256, 384)
        su_f = cst.c(384, 512)
        mincl_f = cst.c(512, 640)
        mstr_f = cst.c(640, 768)
        ident_b = cstb.c(0, 128)
        ones_b = cstb.c(128, 256)
        negm_b = cstb.c(256, 384)

        def vcol(name, i=0, n=1):
            o, _ = VEC[name]
            return vec.c(o + i, o + i + n)

        dma("sp", cst, dv(cst_d, misc_m, 0, 1))
        dma("sp", vec, dv(vec_d, misc_m, 1, 2))
        def convert(a, end):
            while a < end:
                b = min(end, a + 2048)
                dma("pool", V(wbf_d[:, a:b], wbf_m, a, b, BF16), V(w32_d[:, a:b], w32_v.mem, a, b, F32))
                a = b

        convert(0, lsplit)
        op("dve", lambda e: e.tensor_copy(ident_b.ap, ident_f.ap), [ident_f], [ident_b])
        op("dve", lambda e: e.tensor_copy(ones_b.ap, ones_f.ap), [ones_f], [ones_b])
        op("dve", lambda e: e.tensor_copy(negm_b.ap, cst.c(768, 896).ap), [cst], [negm_b])
        op("dve", lambda e: e.memset(halo.ap, 0.0), [], [halo])
        op("dve", lambda e: e.memset(neghalf.ap, -0.5), [], [neghalf])
        op("dve", lambda e: e.memset(lnqs.ap, float(np.log(128 ** -0.5))), [], [lnqs])
        op("dve", lambda e: e.memset(Sst.ap, 0.0), [], [Sst])
        op("dve", lambda e: e.memset(Sbf.ap, 0.0), [], [Sbf])
        op("act", lambda e: e.activation(out=nega.ap, in_=vcol("a_log", 0, 8).ap, func=AF.Exp), [vec], [nega])
        op("dve", lambda e: e.tensor_scalar(nega.ap, nega.ap, -1.0, None, ALU.mult), [nega], [nega])

        ring_i = [0]

        def load_unit(name):
            off, L = lay_off[name]
            slot = ring[ring_i[0] % NSLOT]
            ring_i[0] += 1
            dst = slot.c(0, L)
            dma("sp", dst, V(wbf_d[:, off:off + L], wbf_m, off, off + L, BF16))
            return dst

        rr = {"a": 0}

        def dbg_store(name, src, rows=None, cols=None):
            if name not in dbg_d:
                return
            d = dbg_d[name]
            r0, r1 = rows if rows else (0, d.shape[0])
            c0, c1 = cols if cols else (0, d.shape[1])
            dma("pool", V(d[r0:r1, c0:c1], misc_m, 2, 3, F32), src)

        def rmsnorm_T(src_chunks, nfeat, gain_name, dst_chunks, base, pre=False):
            nchunk = len(src_chunks)
            sr = AR.alloc(TG, F32)
            rstd = AR.alloc(TG, F32)
            acc = ps[5]
            if not pre:
                sq = [AR.alloc(TG, BF16) for _ in range(2)]
                for c in range(nchunk):
                    s = sq[c % 2]
                    op("act", lambda e, s=s, c=c: e.activation(out=s.ap, in_=src_chunks[c].ap, func=AF.Square),
                       [src_chunks[c]], [s])
                    op("pe", lambda e, s=s, c=c: e.matmul(acc.ap, ones_b.ap, s.ap, start=(c == 0), stop=(c == nchunk - 1)),
                       [ones_b, s], [acc])
            op("act", lambda e: e.activation(out=sr.ap, in_=acc.ap, func=AF.Ln, bias=EPS, scale=1.0 / nfeat),
               [acc], [sr])
            op("act", lambda e: e.activation(out=rstd.ap, in_=sr.ap, func=AF.Exp, scale=-0.5), [sr], [rstd])
            for c in range(nchunk):
                g = vcol(gain_name, c)
                op("dve", lambda e, c=c, g=g: e.scalar_tensor_tensor(dst_chunks[c].ap, src_chunks[c].ap, g.ap, rstd.ap,
                                                                    ALU.mult, ALU.mult),
                   [src_chunks[c], vec, rstd], [dst_chunks[c]])
            return rstd

        hTc = [hT.c(c * TG, (c + 1) * TG) for c in range(8)]
        uTc = [uT.c(c * TG, (c + 1) * TG) for c in range(8)]

        def norm_acc_sq(c):
            s_ = nsq[c % 2]
            op("act", lambda e: e.activation(out=s_.ap, in_=hTc[c].ap, func=AF.Square), [hTc[c]], [s_])

        def norm_acc_pe(c):
            s_ = nsq[c % 2]
            op("pe", lambda e: e.matmul(ps[5].ap, ones_b.ap, s_.ap, start=(c == 0), stop=(c == 7)), [ones_b, s_], [ps[5]])

        def ffn(pre, norm_name, pre_acc=False):
            top = AR.top
            rmsnorm_T(hTc, D, norm_name, uTc, 0, pre=pre_acc)
            aT = [AR.alloc(TG, BF16) for _ in range(NF)]
            sg = [AR.alloc(TG, F32) for _ in range(2)]
            for f in range(NF):
                u = load_unit(f"{pre}_gu{f}")
                pg, pu = ps[(f % 2) * 2], ps[(f % 2) * 2 + 1]
                for kc in range(8):
                    wg = u.c(kc * 256, kc * 256 + 128)
                    op("pe", lambda e, wg=wg, kc=kc, pg=pg: e.matmul(pg.ap, wg.ap, uTc[kc].ap, start=(kc == 0), stop=(kc == 7)),
                       [wg, uTc[kc]], [pg])
                for kc in range(8):
                    wu = u.c(kc * 256 + 128, kc * 256 + 256)
                    op("pe", lambda e, wu=wu, kc=kc, pu=pu: e.matmul(pu.ap, wu.ap, uTc[kc].ap, start=(kc == 0), stop=(kc == 7)),
                       [wu, uTc[kc]], [pu])
                s = sg[f % 2]
                op("act", lambda e, s=s, pg=pg: e.activation(out=s.ap, in_=pg.ap, func=AF.Silu), [pg], [s])
                op("dve", lambda e, s=s, pu=pu, f=f: e.tensor_tensor(aT[f].ap, pu.ap, s.ap, ALU.mult), [pu, s], [aT[f]])
            for c in range(8):
                u = load_unit(f"{pre}_d{c}")
                py = ps[4] if c % 2 == 0 else ps[6]
                for f in range(NF):
                    wd = u.c(f * 128, (f + 1) * 128)
                    op("pe", lambda e, wd=wd, f=f, py=py: e.matmul(py.ap, wd.ap, aT[f].ap, start=(f == 0), stop=(f == NF - 1)),
                       [wd, aT[f]], [py])
                op("dve", lambda e, c=c, py=py: e.scalar_tensor_tensor(hTc[c].ap, py.ap, 0.5, hTc[c].ap, ALU.mult, ALU.add),
                   [py, hTc[c]], [hTc[c]])
                if c >= 1:
                    norm_acc_pe(c - 1)
                norm_acc_sq(c)
            norm_acc_pe(7)
            AR.top = top

        def proj_chunk(unit, nk, rhs_chunks, pout, mcols=None, ucol0=0, ustride=None, n=TG, r0=0):
            m = mcols if mcols else 128
            st = ustride if ustride else m
            for kc in range(nk):
                w = unit.c(kc * st + ucol0, kc * st + ucol0 + m)
                rhs = rhs_chunks[kc].c(r0, r0 + n)
                po = pout.pc(0, m, 0, n)
                op("pe", lambda e, w=w, rhs=rhs, kc=kc, po=po: e.matmul(po.ap, w.ap, rhs.ap, start=(kc == 0), stop=(kc == nk - 1)),
                   [w, rhs], [pout])

        def cp(eng, dst, src, rd=None, wr=None):
            rd = rd if rd is not None else [src]
            wr = wr if wr is not None else [dst]
            if eng == "act":
                op("act", lambda e: e.copy(dst.ap, src.ap), rd, wr)
            elif eng == "dve":
                op("dve", lambda e: e.tensor_copy(dst.ap, src.ap), rd, wr)
            else:
                op("pool", lambda e: e.tensor_copy(dst.ap, src.ap), rd, wr)

        def mm(out, lhsT, rhs, start=True, stop=True):
            op("pe", lambda e: e.matmul(out.ap, lhsT.ap, rhs.ap, start=start, stop=stop), [lhsT, rhs], [out])

        def tt(eng, out, a, b, alu):
            op(eng, lambda e: e.tensor_tensor(out.ap, a.ap, b.ap, alu), [a, b], [out])

        def ts(eng, out, a, s1, s2, o0, o1=None):
            rd = [a] + [s for s in (s1, s2) if isinstance(s, V)]
            a1 = s1.ap if isinstance(s1, V) else s1
            a2 = s2.ap if isinstance(s2, V) else s2
            if o1 is None:
                op(eng, lambda e: e.tensor_scalar(out.ap, a.ap, a1, a2, o0), rd, [out])
            else:
                op(eng, lambda e: e.tensor_scalar(out.ap, a.ap, a1, a2, o0, o1), rd, [out])

        def stt(out, a, s, b, o0, o1):
            rd = [a, b] + ([s] if isinstance(s, V) else [])
            sa = s.ap if isinstance(s, V) else s
            op("dve", lambda e: e.scalar_tensor_tensor(out.ap, a.ap, sa, b.ap, o0, o1), rd, [out])

        def act(out, a, func, bias=None, scale=None, accum=None):
            rd = [a] + [s for s in (bias, scale) if isinstance(s, V)]
            wr = [out] + ([accum] if accum is not None else [])
            kw = {}
            if bias is not None:
                kw["bias"] = bias.ap if isinstance(bias, V) else bias
            if scale is not None:
                kw["scale"] = scale.ap if isinstance(scale, V) else scale
            if accum is not None:
                kw["accum_out"] = accum.ap
            op("act", lambda e: e.activation(out=out.ap, in_=a.ap, func=func, **kw), rd, wr)

        def mixer(g, t0):
            top0 = AR.top
            rmsnorm_T(hTc, D, "mix_norm", uTc, 0, pre=True)
            AR.top = top0
            oTn = [AR.alloc(TG, BF16) for _ in range(NH)]
            top1 = AR.top
            P64 = lambda v: v.p(0, 64)
            beta = AR.alloc(32, F32)
            nbeta = AR.alloc(32, F32)
            xa = AR.alloc(32, F32)
            ee = AR.alloc(32, F32)
            spv = AR.alloc(32, F32)
            gv = AR.alloc(32, F32)
            gc = AR.alloc(32, F32)
            gt = AR.alloc(32, F32)
            eg = AR.alloc(32, F32)
            egl = AR.alloc(32, F32)
            egt = AR.alloc(32, F32)
            dl = AR.alloc(32, F32)
            ubg = load_unit("in_bg")
            dtb = vcol("dt_bias", 0, 8)
            for t in range(4):
                pb = ps[0].c(0, 16)
                for kc in range(8):
                    mm(pb, uTc[kc].c(t * 128, (t + 1) * 128), ubg.c(kc * 16, kc * 16 + 16), kc == 0, kc == 7)
                act(beta.c(t * 8, t * 8 + 8), pb.c(0, 8), AF.Sigmoid)
                tt("dve", xa.c(t * 8, t * 8 + 8), pb.c(8, 16), dtb, ALU.add)
            act(ee, xa, AF.Exp)
            act(spv, ee, AF.Ln, bias=1.0, scale=1.0)
            for t in range(4):
                tt("dve", gv.c(t * 8, t * 8 + 8), spv.c(t * 8, t * 8 + 8), nega, ALU.mult)
            ts("dve", nbeta, beta, -1.0, None, ALU.mult)
            for t in range(4):
                pg = ps[1].c(32, 40)
                mm(pg, tri_f, gv.c(t * 8, t * 8 + 8))
                cp("dve", gc.c(t * 8, t * 8 + 8), pg)
                pt = ps[2].c(64, 72)
                mm(pt, ones_f, gv.c(t * 8, t * 8 + 8))
                cp("dve", gt.c(t * 8, t * 8 + 8), pt)
            act(eg, gc, AF.Exp)
            tt("dve", dl, gt, gc, ALU.subtract)
            act(egl, dl, AF.Exp)
            act(egt, gt, AF.Exp)

            NI = 8
            NSET = 6
            sets = []
            for _i in range(NSET):
                sets.append({"cacc": AR.alloc(TG, F32), "hb": AR.alloc(6, F32)})
            nsets = [{"sqb": AR.alloc(TG, BF16), "rs": AR.alloc(TG, F32), "srr": AR.alloc(TG, F32)} for _ in range(2)]
            p1 = {"bank": 0, "set": 0, "nset": 0}

            def p1_bank():
                p1["bank"] += 1
                return ps[p1["bank"] % 8]

            def p1_set():
                p1["set"] += 1
                return sets[p1["set"] % NSET]

            def p1_nset():
                p1["nset"] += 1
                return nsets[p1["nset"] % 2]
            slots = []
            for s in range(NI):
                B = {}
                for nm in ("QnT", "KnT", "VT", "zs"):
                    B[nm] = AR.alloc(TG, BF16)
                for nm in ("Kg", "Kdec", "Vt", "attnT", "Ybf", "w0T", "vnew", "on"):
                    B[nm] = AR.alloc(128, BF16)
                for nm in ("dinc", "dstr", "u0b", "o1", "oo"):
                    B[nm] = AR.alloc(128, F32)
                B["Gm"] = B["dstr"]
                B["decT"] = B["dinc"]
                B["junk"] = B["o1"]
                for nm in ("W", "WT", "Y"):
                    B[nm] = [AR.alloc(128, F32) for _ in range(2)]
                for nm in ("ms", "msr", "rst"):
                    B[nm] = AR.alloc(1, F32)
                B["X"] = ps[s]
                slots.append(B)

            def gdn_phase1(B, h):
                o_c, _ = VEC["conv"]
                S3 = [p1_set(), p1_set(), p1_set()]
                X3 = [p1_bank(), p1_bank(), p1_bank()]
                outs = [B["QnT"], B["KnT"], B["VT"]]
                w3 = []
                for s, nm in enumerate("qkv"):
                    u = load_unit(f"in_{nm}{h}")
                    proj_chunk(u, 8, uTc, X3[s])
                    ch = s * 8 + h
                    w3.append([vec.c(o_c + j * 24 + ch, o_c + j * 24 + ch + 1) for j in range(4)])
                for s in range(3):
                    ch = s * 8 + h
                    hl = halo.c(ch * 3, ch * 3 + 3)
                    hb = S3[s]["hb"]
                    cp("act", hb.c(0, 3), hl)
                    cp("act", hb.c(3, 6), X3[s].c(0, 3))
                    cp("act", hl, X3[s].c(TG - 3, TG))
                for j in range(4):
                    for s in range(3):
                        cacc_ = S3[s]["cacc"]
                        if j == 0:
                            ts("dve", cacc_.c(3, TG), X3[s].c(0, TG - 3), w3[s][0], None, ALU.mult)
                        else:
                            stt(cacc_.c(3, TG), X3[s].c(j, j + TG - 3), w3[s][j], cacc_.c(3, TG), ALU.mult, ALU.add)
                for j in range(4):
                    for s in range(3):
                        cacc_, hb = S3[s]["cacc"], S3[s]["hb"]
                        if j == 0:
                            ts("dve", cacc_.c(0, 3), hb.c(0, 3), w3[s][0], None, ALU.mult)
                        else:
                            stt(cacc_.c(0, 3), hb.c(j, j + 3), w3[s][j], cacc_.c(0, 3), ALU.mult, ALU.add)
                B["p1"] = (S3, outs)

            def gdn_phase1b(B, h):
                S3, outs = B["p1"]
                for s in range(3):
                    act(outs[s], S3[s]["cacc"], AF.Silu)
                Xz = p1_bank()
                u = load_unit(f"in_z{h}")
                proj_chunk(u, 8, uTc, Xz)
                act(B["zs"], Xz, AF.Silu)
                Xn = [p1_bank(), p1_bank()]
                N2 = [p1_nset(), p1_nset()]
                for s in range(2):
                    tt("pool", N2[s]["sqb"], outs[s], outs[s], ALU.mult)
                for s in range(2):
                    mm(Xn[s], ones_b, N2[s]["sqb"])
                for s in range(2):
                    act(N2[s]["srr"], Xn[s], AF.Ln, bias=EPS, scale=1.0)
                for s in range(2):
                    if GDN_P1POOL and s == 0:
                        act(N2[s]["rs"], N2[s]["srr"], AF.Exp, scale=-0.5, bias=lnqs)
                    else:
                        act(N2[s]["rs"], N2[s]["srr"], AF.Exp, scale=-0.5)
                if GDN_P1POOL:
                    tt("pool", B["QnT"], B["QnT"], N2[0]["rs"], ALU.mult)
                    tt("pool", B["KnT"], B["KnT"], N2[1]["rs"], ALU.mult)
                else:
                    stt(B["QnT"], B["QnT"], float(128 ** -0.5), N2[0]["rs"], ALU.mult, ALU.mult)
                    tt("dve", B["KnT"], B["KnT"], N2[1]["rs"], ALU.mult)

            def gdn_head(B, h):
                Sh = Sst.c(h * 128, (h + 1) * 128)
                Sbh = Sbf.c(h * 128, (h + 1) * 128)
                X = B["X"]
                Xb = V(X.ap.bitcast(BF16), X.mem, 0, 2048, BF16, True)
                R = [X.c(i * 128, (i + 1) * 128) for i in range(4)]
                for t in range(4):
                    col = t * 8 + h
                    KnTt = B["KnT"].c(t * 128, (t + 1) * 128)
                    QnTt = B["QnT"].c(t * 128, (t + 1) * 128)
                    VTt = B["VT"].c(t * 128, (t + 1) * 128)
                    pK = Xb.c(768, 896)
                    pV = Xb.c(896, 1024)
                    op("pe", lambda e: e.transpose(pK.ap, KnTt.ap, ident_b.ap), [KnTt, ident_b], [pK])
                    op("pe", lambda e: e.transpose(pV.ap, VTt.ap, ident_b.ap), [VTt, ident_b], [pV])
                    yield
                    kg, kd, vt = B["Kg"], B["Kdec"], B["Vt"]
                    act(kg, pK, AF.Copy, scale=eg.c(col, col + 1))
                    if GDN_BAL >= 1:
                        act(kd, pK, AF.Copy, scale=egl.c(col, col + 1))
                    else:
                        ts("dve", kd, pK, egl.c(col, col + 1), None, ALU.mult)
                    cp("act", vt, pV)
                    Gm = B["Gm"]
                    if GDN_BAL >= 3:
                        act(Gm, su_f, AF.Copy, scale=gv.c(col, col + 1))
                    else:
                        ts("dve", Gm, su_f, gv.c(col, col + 1), None, ALU.mult)
                    yield
                    pKK, pQK, pD = R[0], R[1], R[2]
                    mm(pKK, KnTt, KnTt)
                    mm(pQK, KnTt, QnTt)
                    mm(pD, Gm, tri_f)
                    yield
                    decT, dinc, dstr = B["decT"], B["dinc"], B["dstr"]
                    act(decT, pD, AF.Exp)
                    yield
                    tt("pool", dinc, decT, mincl_f, ALU.mult)
                    tt("pool", dstr, dinc, mstr_f, ALU.mult)
                    yield
                    at = B["attnT"]
                    W, WT, Y = B["W"][0], B["WT"][0], B["Y"][0]
                    stt(W, pKK, nbeta.c(col, col + 1), dstr, ALU.mult, ALU.mult)
                    tt("dve", at, pQK, dinc, ALU.mult)
                    yield
                    pT = R[0]
                    op("pe", lambda e: e.transpose(pT.ap, W.ap, ident_f.ap), [W, ident_f], [pT])
                    tt("pool", Y, W, ident_f, ALU.add)
                    yield
                    cp("act", WT, pT)
                    yield
                    cur = 0
                    pA, pB, pC = R[1], R[2], R[3]
                    for k in range(1, 7):
                        nxt = 1 - cur
                        Wc, WTc, Yc = B["W"][cur], B["WT"][cur], B["Y"][cur]
                        Wn, WTn, Yn = B["W"][nxt], B["WT"][nxt], B["Y"][nxt]
                        mm(pB, Wc, WTc)
                        if k <= 5:
                            mm(pA, WTc, Wc)
                        yield
                        cp("act" if (GDN_BAL >= 2 and k % 2 == 0) else "dve", WTn, pB)
                        if k <= 5:
                            cp("act", Wn, pA)
                        yield
                        mm(pC, WTn, Yc)
                        yield
                        tt("dve", Yn, pC, Yc, ALU.add)
                        yield
                        cur = nxt
                    Ybf = B["Ybf"]
                    cp("act", Ybf, B["Y"][cur])
                    yield
                    pu0, pw0 = R[0], R[1]
                    mm(pu0, Ybf, vt)
                    mm(pw0, kg, Ybf)
                    yield
                    ub_, wt_ = B["u0b"], B["w0T"]
                    act(ub_, pu0, AF.Copy, scale=beta.c(col, col + 1))
                    cp("act" if GDN_BAL >= 1 else "dve", wt_, pw0)
                    yield
                    pwS, pQS = R[2], R[3]
                    mm(pwS, wt_, Sbh)
                    mm(pQS, QnTt, Sbh)
                    yield
                    vnew = B["vnew"]
                    stt(vnew, pwS, nbeta.c(col, col + 1), ub_, ALU.mult, ALU.add)
                    o1 = B["o1"]
                    act(o1, pQS, AF.Copy, scale=eg.c(col, col + 1))
                    yield
                    pAV, pKV = R[0], R[1]
                    mm(pAV, at, vnew)
                    mm(pKV, kd, vnew)
                    yield
                    oo = B["oo"]
                    tt("dve", oo, pAV, o1, ALU.add)
                    stt(Sh, Sh, egt.c(col, col + 1), pKV, ALU.mult, ALU.add)
                    yield
                    cp("act", Sbh, Sh)
                    ms, msr, rst, on = B["ms"], B["msr"], B["rst"], B["on"]
                    junk = B["junk"]
                    if GDN_BAL >= 3:
                        act(junk, oo, AF.Square, accum=ms)
                    else:
                        op("dve", lambda e: e.scalar_tensor_tensor(junk.ap, oo.ap, 1.0, oo.ap, ALU.mult, ALU.mult, accum_out=ms.ap),
                           [oo], [junk, ms])
                    yield
                    ts("pool", msr, ms, 1.0 / 128, EPS, ALU.mult, ALU.add)
                    tt("pool", rst, msr, neghalf, ALU.pow)
                    yield
                    act(on, oo, AF.Copy, scale=rst)
                    yield
                    pO = Xb.c(512, 640)
                    op("pe", lambda e: e.transpose(pO.ap, on.ap, ident_b.ap), [on, ident_b], [pO])
                    yield
                    stt(oTn[h].c(t * 128, (t + 1) * 128), pO, vcol("gdn_norm"), B["zs"].c(t * 128, (t + 1) * 128), ALU.mult, ALU.mult)
                    yield

            gdn_phase1(slots[0], 0)
            for s in range(NI):
                if s + 1 < NI:
                    gdn_phase1(slots[s + 1], s + 1)
                gdn_phase1b(slots[s], s)
            gens = [gdn_head(slots[s], s) for s in range(NI)]
            for s in range(NI):
                for _ in range(GDN_STAGGER[s % len(GDN_STAGGER)]):
                    next(gens[s])
            while gens:
                for gg in list(gens):
                    try:
                        next(gg)
                    except StopIteration:
                        gens.remove(gg)
            AR.top = top1
            if g == 0 and lsplit < ltot:
                convert(lsplit, ltot)
            oBT = [AR.alloc(TG, BF16) for _ in range(NH)]
            top1b = AR.top
            QnopeT = [AR.alloc(TG, BF16) for _ in range(NH)]
            QropeT = [AR.alloc(TG, BF16) for _ in range(NH)]
            NCH = 4
            kch = [AR.alloc(TG, BF16) for _ in range(NCH)]
            vch = [AR.alloc(TG, BF16) for _ in range(NCH)]
            pbuf = [AR.alloc(TG, BF16) for _ in range(3)]
            rl = AR.alloc(TG, F32)
            top2 = AR.top
            qd = [AR.alloc(TG, F32) for _ in range(3)]
            qdn = [AR.alloc(TG, BF16) for _ in range(3)]
            ckv = [AR.alloc(TG, F32) for _ in range(2)]
            ckvn = [AR.alloc(TG, BF16) for _ in range(2)]
            for i in range(3):
                u = load_unit(f"in_qd{i}")
                proj_chunk(u, 8, uTc, ps[i % 2])
                cp("act", qd[i], ps[i % 2])
            for i in range(2):
                u = load_unit(f"in_ckv{i}")
                proj_chunk(u, 8, uTc, ps[2 + i])
                cp("act", ckv[i], ps[2 + i])
            topn = AR.top
            rmsnorm_T(qd, QL, "q_a_norm", qdn, 0)
            AR.top = topn
            rmsnorm_T(ckv, KVL, "kv_a_norm", ckvn, 0)
            AR.top = topn
            posi = AR.alloc(TG, I32)
            ang = AR.alloc(TG, F32)
            tq = AR.alloc(TG, F32)
            ni = AR.alloc(TG, I32)
            nf = AR.alloc(TG, F32)
            rr_ = AR.alloc(TG, F32)
            dd = AR.alloc(TG, F32)
            dma("sp", P64(posi), V(pos_d[:, t0:t0 + TG], misc_m, 4, 5, I32))
            cp("dve", P64(tq), P64(posi))
            ts("dve", P64(ang), P64(tq), vcol("invfreq").p(0, 64), None, ALU.mult)
            C1 = 6.28125
            C2 = float(2 * np.pi - 6.28125)
            a_ = P64(ang)
            ts("dve", P64(tq), a_, float(1.0 / (2 * np.pi)), None, ALU.mult)
            cp("dve", P64(ni), P64(tq))
            cp("dve", P64(nf), P64(ni))
            stt(P64(rr_), P64(nf), -C1, a_, ALU.mult, ALU.add)
            stt(P64(rr_), P64(nf), -C2, P64(rr_), ALU.mult, ALU.add)
            for which in range(2):
                if which == 1:
                    ts("dve", P64(rr_), P64(rr_), PI / 2, None, ALU.add)
                ts("dve", P64(dd), P64(rr_), PI, float(-2 * np.pi), ALU.is_gt, ALU.mult)
                tt("dve", P64(rr_), P64(rr_), P64(dd), ALU.add)
                ts("dve", P64(dd), P64(rr_), -PI, float(2 * np.pi), ALU.is_lt, ALU.mult)
                tt("dve", P64(rr_), P64(rr_), P64(dd), ALU.add)
                ts("dve", P64(rr_), P64(rr_), PI, -PI, ALU.min, ALU.max)
                if which == 0:
                    act(P64(tq), P64(rr_), AF.Sin)
                    ts("dve", s2, P64(tq), vcol("sgn").p(0, 64), None, ALU.mult)
                else:
                    act(c2, P64(rr_), AF.Sin)
            AR.top = topn
            t1 = AR.alloc(TG, F32)
            t2 = AR.alloc(TG, F32)
            u = load_unit("in_kpe")
            P1 = ps[0]
            P2 = ps[1]
            proj_chunk(u, 8, uTc, P1, mcols=64, ucol0=0, ustride=128)
            proj_chunk(u, 8, uTc, P2, mcols=64, ucol0=64, ustride=128)
            tt("dve", P64(t1), P1.p(0, 64), c2, ALU.mult)
            tt("dve", P64(t2), P2.p(0, 64), s2, ALU.mult)
            tt("pool", kpe.c(t0, t0 + TG), P64(t1), P64(t2), ALU.add)
            Kst = [AR.alloc(TG, BF16) for _ in range(2)]
            uk = load_unit("kv_k")
            for h in range(NH):
                P = ps[2 + (h % 2)]
                for kc in range(2):
                    mm(P, uk.c(kc * 1024 + h * 128, kc * 1024 + h * 128 + 128), ckvn[kc], kc == 0, kc == 1)
                ks = Kst[h % 2]
                cp("act" if h % 2 == 0 else "dve", ks, P)
                dma("act", V(kc_d[h, :, t0:t0 + TG], kc_m[h], t0, t0 + TG, BF16), ks)
            Vst = [AR.alloc(TG, BF16) for _ in range(2)]
            uv = load_unit("kv_v")
            for t in range(4):
                tile_i = g * 4 + t
                for half in range(2):
                    P = ps[2 + half]
                    for kc in range(2):
                        mm(P, ckvn[kc].c(t * 128, (t + 1) * 128), uv.c(kc * 1024 + half * 512, kc * 1024 + half * 512 + 512), kc == 0, kc == 1)
                    vs = Vst[half]
                    cp("act" if half == 0 else "dve", vs, P)
                    for j in range(4):
                        hh = half * 4 + j
                        dma("act", V(vc_d[hh, :, tile_i, :], vc_m[hh], tile_i, tile_i + 1, BF16), vs.c(j * 128, (j + 1) * 128))
            def q_proj(h):
                u = load_unit(f"q_up{h}")
                b0 = 3 * (h % 2)
                proj_chunk(u, 3, qdn, ps[b0], mcols=128, ucol0=0, ustride=256)
                proj_chunk(u, 3, qdn, ps[b0 + 1], mcols=64, ucol0=128, ustride=256)
                proj_chunk(u, 3, qdn, ps[b0 + 2], mcols=64, ucol0=192, ustride=256)

            def q_evac(h):
                b0 = 3 * (h % 2)
                ta, tb = t12[h % 2]
                act(QnopeT[h], ps[b0], AF.Copy, scale=SCALE)
                stt(P64(ta), ps[b0 + 1].p(0, 64), SCALE, c2, ALU.mult, ALU.mult)
                stt(P64(tb), ps[b0 + 2].p(0, 64), SCALE, s2, ALU.mult, ALU.mult)
                tt("pool", P64(QropeT[h]), P64(ta), P64(tb), ALU.add)

            t12 = [(t1, t2), (AR.alloc(TG, F32), AR.alloc(TG, F32))]
            q_proj(0)
            for h in range(NH):
                if h + 1 < NH:
                    q_proj(h + 1)
                q_evac(h)
            AR.top = top2

            nkt = (g + 1) * 4
            pairs = [(h, kt) for h in range(NH) for kt in range(nkt)]
            chunk_of = {}
            nchunk = [0]

            def a_qk(i):
                h, kt = pairs[i]
                if kt % 4 == 0:
                    ci = nchunk[0] % NCH
                    nchunk[0] += 1
                    chunk_of[(h, kt // 4)] = ci
                    k0 = kt * 128
                    dma("sp", kch[ci], V(kc_d[h, :, k0:k0 + TG], kc_m[h], k0, k0 + TG, BF16))
                    dma("sp", vch[ci], V(vc_d[h, :, kt:kt + 4, :], vc_m[h], kt, kt + 4, BF16),
                        out_ap=vch[ci].ap.rearrange("p (t d) -> p t d", d=128))
                ci = chunk_of[(h, kt // 4)]
                kk = kt % 4
                j = kt - g * 4
                q0 = max(0, j) * 128
                SB = ps[i % 3].c(q0, TG)
                mm(SB, kch[ci].c(kk * 128, (kk + 1) * 128), QnopeT[h].c(q0, TG), True, False)
                mm(SB, kpe.c(kt * 128, (kt + 1) * 128), P64(QropeT[h]).c(q0, TG), False, j < 0)
                if j >= 0:
                    mm(ps[i % 3].c(q0, q0 + 128), ident_b, negm_b, False, True)

            def a_pv(i):
                h, kt = pairs[i]
                ci = chunk_of[(h, kt // 4)]
                kk = kt % 4
                OB, LB = (ps[4], ps[5]) if h % 2 == 0 else (ps[6], ps[7])
                j = kt - g * 4
                q0 = max(0, j) * 128
                SB = ps[i % 3].c(q0, TG)
                pb_ = pbuf[i % 3].c(q0, TG)
                act(pb_, SB, AF.Exp)
                mm(OB.c(q0, TG), vch[ci].c(kk * 128, (kk + 1) * 128), pb_, kt == 0, kt == nkt - 1)
                mm(LB.c(q0, TG), ones_b, pb_, kt == 0, kt == nkt - 1)
                if kt == nkt - 1:
                    op("dve", lambda e: e.reciprocal(rl.ap, LB.ap), [LB], [rl])
                    tt("dve", oBT[h], OB, rl, ALU.mult)

            def attention_gen():
                a_qk(0)
                for i in range(len(pairs)):
                    if i + 1 < len(pairs):
                        a_qk(i + 1)
                    a_pv(i)
                    yield

            att = [attention_gen()]

            def att_step(n=1):
                for _ in range(n):
                    if att[0] is None:
                        return
                    try:
                        next(att[0])
                    except StopIteration:
                        att[0] = None

            att_step(100000)
            AR.top = top1b
            sgA = [AR.alloc(TG, F32) for _ in range(2)]
            mA = [AR.alloc(TG, F32) for _ in range(2)]
            mB = [AR.alloc(TG, F32) for _ in range(2)]
            mT = [AR.alloc(TG, BF16) for _ in range(8)]
            for c in range(8):
                ua = load_unit(f"proj_a{c}")
                Pa = ps[0]
                for h in range(NH):
                    mm(Pa, ua.c(h * 128, (h + 1) * 128), oTn[h], h == 0, h == NH - 1)
                ug = load_unit(f"in_ga{c}")
                Pg = ps[1]
                proj_chunk(ug, 8, uTc, Pg)
                if g == 0:
                    cp("act", sgA[0], Pa)
                    dbg_store("yA", sgA[0], rows=(c * 128, (c + 1) * 128))
                act(sgA[c % 2], Pg, AF.Sigmoid)
                tt("dve", mA[c % 2], Pa, sgA[c % 2], ALU.mult)
                ub2 = load_unit(f"proj_b{c}")
                Pb = ps[2]
                for h in range(NH):
                    mm(Pb, ub2.c(h * 128, (h + 1) * 128), oBT[h], h == 0, h == NH - 1)
                ug2 = load_unit(f"in_gb{c}")
                Pg2 = ps[3]
                proj_chunk(ug2, 8, uTc, Pg2)
                if g == 0:
                    cp("act", mB[0], Pb)
                    dbg_store("yB", mB[0], rows=(c * 128, (c + 1) * 128))
                act(sgA[(c + 1) % 2], Pg2, AF.Sigmoid)
                tt("dve", mB[c % 2], Pb, sgA[(c + 1) % 2], ALU.mult)
                tt("pool", mT[c], mA[c % 2], mB[c % 2], ALU.add)
            for c in range(8):
                uo = load_unit(f"w_o{c}")
                Po = ps[4] if c % 2 == 0 else ps[6]
                for k in range(8):
                    mm(Po, uo.c(k * 128, (k + 1) * 128), mT[k], k == 0, k == 7)
                tt("dve", hTc[c], Po, hTc[c], ALU.add)
                if c >= 1:
                    norm_acc_pe(c - 1)
                norm_acc_sq(c)
            norm_acc_pe(7)
            AR.top = top0

        for g in range(ngroups):
            t0 = g * TG
            topA = AR.top
            xin = [AR.alloc(D, F32) for _ in range(2)]
            for t in range(4):
                xi = xin[t % 2]
                r0 = t0 + t * 128
                dma("sp", xi, V(x_d[r0:r0 + 128, :], x_m, r0, r0 + 128, F32))
                for half in range(2):
                    pb = ps[6 + half]
                    for j in range(4):
                        c = half * 4 + j
                        src = xi.c(c * 128, (c + 1) * 128)
                        dst = pb.c(j * 128, (j + 1) * 128)
                        op("pe", lambda e, src=src, dst=dst: e.transpose(dst.ap, src.ap, ident_f.ap), [src, ident_f], [dst])
                    o_ap = hT.ap[:, half * 4 * TG:(half * 4 + 4) * TG].rearrange("p (c t) -> p c t", c=4)[:, :, t * 128:(t + 1) * 128]
                    i_ap = pb.ap.rearrange("p (c t) -> p c t", c=4)
                    hv = V(o_ap, hT.mem, half * 4 * TG * 4, (half * 4 + 4) * TG * 4, F32)
                    eng = "act" if half == 0 else "dve"
                    if eng == "act":
                        op("act", lambda e, o_ap=o_ap, i_ap=i_ap: e.copy(o_ap, i_ap), [pb], [hv])
                    else:
                        op("dve", lambda e, o_ap=o_ap, i_ap=i_ap: e.tensor_copy(o_ap, i_ap), [pb], [hv])
            AR.top = topA
            ffn("ffn1", "ffn1_norm")
            if g == 0:
                for c in range(8):
                    dbg_store("h1", hTc[c], rows=(c * 128, (c + 1) * 128))
            if MIXER_ENABLED:
                mixer(g, t0)
            ffn("ffn2", "ffn2_norm", pre_acc=MIXER_ENABLED)
            top = AR.top
            yT = [AR.alloc(TG, F32) for _ in range(8)]
            sr = AR.alloc(TG, F32)
            rstd = AR.alloc(TG, F32)
            acc = ps[5]
            op("act", lambda e: e.activation(out=sr.ap, in_=acc.ap, func=AF.Ln, bias=EPS, scale=1.0 / D), [acc], [sr])
            op("act", lambda e: e.activation(out=rstd.ap, in_=sr.ap, func=AF.Exp, scale=-0.5), [sr], [rstd])
            for c in range(8):
                gcol = vcol("final_norm", c)
                op("dve", lambda e, c=c, gcol=gcol: e.scalar_tensor_tensor(yT[c].ap, hTc[c].ap, gcol.ap, rstd.ap, ALU.mult, ALU.mult),
                   [hTc[c], vec, rstd], [yT[c]])
            ot = [AR.alloc(D, F32) for _ in range(2)]
            for t in range(4):
                o = ot[t % 2]
                for half in range(2):
                    pb = ps[6 + half]
                    for j in range(4):
                        c = half * 4 + j
                        src = yT[c].c(t * 128, (t + 1) * 128)
                        dst = pb.c(j * 128, (j + 1) * 128)
                        op("pe", lambda e, src=src, dst=dst: e.transpose(dst.ap, src.ap, ident_f.ap), [src, ident_f], [dst])
                    od = o.c(half * 512, (half + 1) * 512)
                    if half == 0:
                        op("act", lambda e, od=od, pb=pb: e.copy(od.ap, pb.ap), [pb], [od])
                    else:
                        op("dve", lambda e, od=od, pb=pb: e.tensor_copy(od.ap, pb.ap), [pb], [od])
                r0 = t0 + t * 128
                dma("pool", V(out_d[r0:r0 + 128, :], out_m, r0, r0 + 128, F32), o)
            AR.top = top

        deps = {}
        out_m.deps(0, T, True, deps)
        misc_m.deps(0, 16, True, deps)
        kb.wait_all("pool", list(deps.items()))
    return nc


_CACHE = {}


def kernel(**inputs):
    inp = {k: np.asarray(v) for k, v in inputs.items()}
    lay = build_layout(inp)
    w32 = np.ascontiguousarray(np.concatenate(lay.parts, axis=1))
    vecs = build_vecs(inp)
    consts = build_consts()
    key = "prog"
    if key not in _CACHE:
        _CACHE[key] = build_program(lay.off, lay.n, vecs.shape[1], lsplit=lay.split)
    nc = _CACHE[key]
    x = inp["x"]
    pos = inp["positions"]
    in_maps = []
    for b in range(8):
        in_maps.append({
            "x": np.ascontiguousarray(x[b]),
            "pos": np.ascontiguousarray(np.broadcast_to(pos[b][None, :], (64, T))).astype(np.int32),
            "w32": w32, "vecs": vecs, "consts": consts,
        })
    res = run_bass_kernel_spmd(nc, in_maps, core_ids=list(range(8)))
    out = np.stack([np.asarray(r["out"]) for r in res.results], axis=0)
    return out.astype(np.float32)
```
